# Optimizing a Trainium2 kernel written in Bass

```python
import math
import jax, jax.numpy as jnp
from jax import lax
import numpy as np

D_MODEL = 4096
BATCH = 4
SEQ = 2048
DEPTH = 2

N_MIXERS = 2
EXPAND = 2
D_INNER = EXPAND * D_MODEL
D_MEM_BRANCH = D_INNER // 4
D_MIX = D_INNER - D_MEM_BRANCH
MEM_LEN = 256
MEM_HEADS = 4
MEM_HEAD_DIM = D_MEM_BRANCH // MEM_HEADS
POOL_WINDOWS = (2, 4, 8, 16)
POOL_GROUP = D_MIX // len(POOL_WINDOWS)
DIFF_QK_DIM = 128
DIFF_V_DIM = 2 * DIFF_QK_DIM
DIFF_HEADS = D_MIX // DIFF_V_DIM
Q_BLOCK = 128
RMS_EPS = 1e-6
SUBLN_EPS = 1e-5
IN_POOL = D_MIX + D_MEM_BRANCH + D_INNER
IN_DIFF = 3 * D_MIX + D_MEM_BRANCH + D_INNER

kernel_name = "hybrid_pool_diffattn_gated_memory_trunk"


def rmsnorm(x, g, eps=RMS_EPS):
    xf = x.astype(jnp.float32)
    y = xf * lax.rsqrt(jnp.mean(xf * xf, axis=-1, keepdims=True) + eps)
    return (y * g.astype(jnp.float32)).astype(x.dtype)


def alibi_slopes(n):
    def pow2(m):
        start = 2.0 ** (-8.0 / m)
        return [start ** (i + 1) for i in range(m)]
    if math.log2(n).is_integer():
        s = pow2(n)
    else:
        c = 2 ** math.floor(math.log2(n))
        s = pow2(c) + pow2(2 * c)[0::2][: n - c]
    return np.asarray(s, dtype=np.float32)


def lambda_init_fn(layer_idx):
    return 0.8 - 0.6 * math.exp(-0.3 * layer_idx)


def pool_mixer(u, pool_w, pool_scale):
    B, S, _ = u.shape
    uf = u.astype(jnp.float32)
    cs = jnp.cumsum(uf, axis=1)
    t = jnp.arange(S)
    outs = []
    for g, w in enumerate(POOL_WINDOWS):
        sl = slice(g * POOL_GROUP, (g + 1) * POOL_GROUP)
        c = cs[..., sl]
        prev = jnp.pad(c, ((0, 0), (w, 0), (0, 0)))[:, :S]
        cnt = jnp.minimum(t + 1, w).astype(jnp.float32)[None, :, None]
        outs.append((c - prev) / cnt - uf[..., sl])
    pooled = jnp.stack(outs, axis=2).astype(u.dtype)
    mixed = jnp.einsum('bsgc,gcd->bsgd', pooled, pool_w)
    return mixed.reshape(B, S, D_MIX) * pool_scale


def diff_attention(q, k, v, lam, lambda_init, subln_g):
    B, S, H, _, dk = q.shape
    nb = S // Q_BLOCK
    slopes = jnp.asarray(alibi_slopes(H))
    qb = q.reshape(B, nb, Q_BLOCK, H, 2, dk).transpose(1, 0, 2, 3, 4, 5)
    kpos = jnp.arange(S)
    scale = DIFF_QK_DIM ** -0.5

    def block(args):
        qi, i = args
        qpos = i * Q_BLOCK + jnp.arange(Q_BLOCK)
        s = jnp.einsum('bqhjd,bkhjd->bhjqk', qi, k).astype(jnp.float32) * scale
        dist = (qpos[:, None] - kpos[None, :]).astype(jnp.float32)
        s = s - slopes[None, :, None, None, None] * dist
        s = jnp.where(dist >= 0, s, -jnp.inf)
        p = jax.nn.softmax(s, axis=-1)
        a = p[:, :, 0] - lam * p[:, :, 1]
        return jnp.einsum('bhqk,bkhd->bqhd', a.astype(v.dtype), v)

    o = lax.map(block, (qb, jnp.arange(nb)))
    o = o.transpose(1, 0, 2, 3, 4).reshape(B, S, H, DIFF_V_DIM)
    o = rmsnorm(o, subln_g, SUBLN_EPS) * (1.0 - lambda_init)
    return o.reshape(B, S, H * DIFF_V_DIM)


def mem_attention(qm, mem, mem_norm_g, w_mem_kv):
    B, S, _ = qm.shape
    kv = jnp.einsum('bmd,de->bme', rmsnorm(mem, mem_norm_g), w_mem_kv)
    km = kv[..., :D_MEM_BRANCH].reshape(B, -1, MEM_HEADS, MEM_HEAD_DIM)
    vm = kv[..., D_MEM_BRANCH:].reshape(B, -1, MEM_HEADS, MEM_HEAD_DIM)
    qh = qm.reshape(B, S, MEM_HEADS, MEM_HEAD_DIM)
    s = jnp.einsum('bqhd,bkhd->bhqk', qh, km).astype(jnp.float32) * MEM_HEAD_DIM ** -0.5
    p = jax.nn.softmax(s, axis=-1)
    o = jnp.einsum('bhqk,bkhd->bqhd', p.astype(vm.dtype), vm)
    return o.reshape(B, S, D_MEM_BRANCH)


def pool_layer(x, mem, norm_g, w_in, pool_w, pool_scale, mem_norm_g, w_mem_kv, w_out):
    h = rmsnorm(x, norm_g)
    proj = jnp.einsum('bsd,de->bse', h, w_in)
    u = proj[..., :D_MIX]
    qm = proj[..., D_MIX:D_MIX + D_MEM_BRANCH]
    z = proj[..., D_MIX + D_MEM_BRANCH:]
    y = jnp.concatenate([pool_mixer(u, pool_w, pool_scale),
                         mem_attention(qm, mem, mem_norm_g, w_mem_kv)], axis=-1)
    return x + jnp.einsum('bse,ed->bsd', y * jax.nn.silu(z), w_out)


def diff_layer(x, mem, layer_idx, norm_g, w_in, lq1, lk1, lq2, lk2, subln_g,
               mem_norm_g, w_mem_kv, w_out):
    B, S, _ = x.shape
    h = rmsnorm(x, norm_g)
    proj = jnp.einsum('bsd,de->bse', h, w_in)
    q = proj[..., :D_MIX].reshape(B, S, DIFF_HEADS, 2, DIFF_QK_DIM)
    k = proj[..., D_MIX:2 * D_MIX].reshape(B, S, DIFF_HEADS, 2, DIFF_QK_DIM)
    v = proj[..., 2 * D_MIX:3 * D_MIX].reshape(B, S, DIFF_HEADS, DIFF_V_DIM)
    qm = proj[..., 3 * D_MIX:3 * D_MIX + D_MEM_BRANCH]
    z = proj[..., 3 * D_MIX + D_MEM_BRANCH:]
    lam_init = lambda_init_fn(layer_idx)
    f32 = jnp.float32
    lam = (jnp.exp(jnp.sum(lq1.astype(f32) * lk1.astype(f32)))
           - jnp.exp(jnp.sum(lq2.astype(f32) * lk2.astype(f32))) + lam_init)
    y = jnp.concatenate([diff_attention(q, k, v, lam, lam_init, subln_g),
                         mem_attention(qm, mem, mem_norm_g, w_mem_kv)], axis=-1)
    return x + jnp.einsum('bse,ed->bsd', y * jax.nn.silu(z), w_out)


def setup_inputs(seed: int = 0) -> dict:
    key = jax.random.key(seed)
    ks = jax.random.split(key, 24)
    f32 = jnp.float32
    nrm = lambda k, shape, s: jax.random.normal(k, shape, f32) * s
    gain = lambda k, n: 1.0 + 0.02 * jax.random.normal(k, (n,), f32)
    return {
        "x": jax.random.normal(ks[0], (BATCH, SEQ, D_MODEL), f32),
        "mem": jax.random.normal(ks[1], (BATCH, MEM_LEN, D_MODEL), f32),
        "l0_norm_g": gain(ks[2], D_MODEL),
        "l0_w_in": nrm(ks[3], (D_MODEL, IN_POOL), D_MODEL ** -0.5),
        "l0_pool_w": nrm(ks[4], (len(POOL_WINDOWS), POOL_GROUP, POOL_GROUP), POOL_GROUP ** -0.5),
        "l0_pool_scale": gain(ks[5], D_MIX),
        "l0_mem_norm_g": gain(ks[6], D_MODEL),
        "l0_w_mem_kv": nrm(ks[7], (D_MODEL, 2 * D_MEM_BRANCH), D_MODEL ** -0.5),
        "l0_w_out": nrm(ks[8], (D_INNER, D_MODEL), D_INNER ** -0.5),
        "l1_norm_g": gain(ks[9], D_MODEL),
        "l1_w_in": nrm(ks[10], (D_MODEL, IN_DIFF), D_MODEL ** -0.5),
        "l1_lambda_q1": nrm(ks[11], (DIFF_QK_DIM,), 0.1),
        "l1_lambda_k1": nrm(ks[12], (DIFF_QK_DIM,), 0.1),
        "l1_lambda_q2": nrm(ks[13], (DIFF_QK_DIM,), 0.1),
        "l1_lambda_k2": nrm(ks[14], (DIFF_QK_DIM,), 0.1),
        "l1_subln_g": gain(ks[15], DIFF_V_DIM),
        "l1_mem_norm_g": gain(ks[16], D_MODEL),
        "l1_w_mem_kv": nrm(ks[17], (D_MODEL, 2 * D_MEM_BRANCH), D_MODEL ** -0.5),
        "l1_w_out": nrm(ks[18], (D_INNER, D_MODEL), D_INNER ** -0.5),
        "final_norm_g": gain(ks[19], D_MODEL),
    }


def reference(x, mem, l0_norm_g, l0_w_in, l0_pool_w, l0_pool_scale, l0_mem_norm_g,
              l0_w_mem_kv, l0_w_out, l1_norm_g, l1_w_in, l1_lambda_q1, l1_lambda_k1,
              l1_lambda_q2, l1_lambda_k2, l1_subln_g, l1_mem_norm_g, l1_w_mem_kv,
              l1_w_out, final_norm_g):
    pool_params = [(l0_norm_g, l0_w_in, l0_pool_w, l0_pool_scale, l0_mem_norm_g,
                    l0_w_mem_kv, l0_w_out)]
    diff_params = [(l1_norm_g, l1_w_in, l1_lambda_q1, l1_lambda_k1, l1_lambda_q2,
                    l1_lambda_k2, l1_subln_g, l1_mem_norm_g, l1_w_mem_kv, l1_w_out)]
    for i in range(DEPTH):
        if i % N_MIXERS == 0:
            x = pool_layer(x, mem, *pool_params[i // N_MIXERS])
        else:
            x = diff_layer(x, mem, i, *diff_params[i // N_MIXERS])
    return rmsnorm(x, final_norm_g)
```

```python
import math
from contextlib import ExitStack

import numpy as np
import concourse.bass as bass
import concourse.mybir as mybir
from concourse.bass_utils import run_bass_kernel_spmd

F32 = mybir.dt.float32
BF16 = mybir.dt.bfloat16
ALU = mybir.AluOpType
AF = mybir.ActivationFunctionType
AX = mybir.AxisListType

PE, ACT, DVE, POOL, SP = "tensor", "scalar", "vector", "gpsimd", "sync"
COMPUTE = (PE, ACT, DVE, POOL)
ALL_ENG = (PE, ACT, DVE, POOL, SP)
SAME_ENG_SYNC = {PE: False, ACT: True, DVE: True, POOL: True}

HALO = 16
RMS_EPS = 1e-6
SUBLN_EPS = 1e-5
BIG = 1.0e9


class Res:
    __slots__ = ("name", "w", "r")

    def __init__(self, name=""):
        self.name = name
        self.w = None
        self.r = []


class DSem:
    __slots__ = ("sem", "count", "name", "unit", "last")

    def __init__(self, name, unit=16):
        self.name = name
        self.sem = None
        self.count = 0
        self.unit = unit
        self.last = None


class Op:
    __slots__ = ("eng", "emit", "deps", "signal", "sigval", "dsem", "dprev", "dval")

    def __init__(self, eng, emit):
        self.eng = eng
        self.emit = emit
        self.deps = []
        self.signal = False
        self.sigval = None
        self.dsem = None
        self.dprev = 0
        self.dval = 0


class Prog:
    def __init__(self, nc):
        self.nc = nc
        self.ops = {e: [] for e in ALL_ENG}
        self.dsems = []
        self.pending = {e: [] for e in ALL_ENG}

    def dsem(self, name, unit=16):
        d = DSem(name, unit)
        self.dsems.append(d)
        return d

    def _dep(self, op, p):
        if p is None or p is op:
            return
        if p.dsem is None:
            if p.eng == op.eng and not SAME_ENG_SYNC[p.eng] and op.dsem is None:
                return
            p.signal = True
        op.deps.append(p)

    def add(self, eng, emit, reads=(), writes=(), dsem=None, ndma=1):
        op = Op(eng, emit)
        if dsem is not None:
            op.dsem = dsem
            op.dprev = dsem.count * dsem.unit
            dsem.count += ndma
            op.dval = dsem.count * dsem.unit
            dsem.last = op
        if self.pending[eng]:
            for p in self.pending[eng]:
                self._dep(op, p)
            self.pending[eng] = []
        for r in reads:
            self._dep(op, r.w)
        for w in writes:
            self._dep(op, w.w)
            for q in w.r:
                self._dep(op, q)
        for r in reads:
            if op.dsem is None:
                r.r = [q for q in r.r if not (q.dsem is None and q.eng == eng)]
            r.r.append(op)
        for w in writes:
            w.w = op
            w.r = []
        self.ops[eng].append(op)
        return op

    def barrier(self, engines=(PE, ACT, DVE, SP)):
        tails = []
        for e in engines:
            if self.ops[e]:
                last = None
                for o in reversed(self.ops[e]):
                    if o.dsem is None:
                        last = o
                        break
                if last is not None:
                    tails.append(last)
        for d in self.dsems:
            if d.last is not None and d.last.eng in engines:
                tails.append(d.last)
        for e in engines:
            self.pending[e] = list(tails)

    def emit_all(self, final_waits=()):
        nc = self.nc
        with ExitStack() as es:
            esem = {e: es.enter_context(nc.semaphore("s_" + e)) for e in COMPUTE}
            for i, d in enumerate(self.dsems):
                d.sem = es.enter_context(nc.semaphore("d%d_%s" % (i, d.name)))
            for e in COMPUTE:
                c = 0
                for op in self.ops[e]:
                    if op.signal and op.dsem is None:
                        c += 1
                        op.sigval = c
            block = es.enter_context(nc.Block())
            prog = self

            def run(engname, eng):
                waited = {}

                def need(sem, val):
                    if val <= 0:
                        return
                    k = id(sem)
                    if waited.get(k, 0) >= val:
                        return
                    waited[k] = val
                    eng.wait_ge(sem, val)

                for op in prog.ops[engname]:
                    for p in op.deps:
                        if p.dsem is not None:
                            need(p.dsem.sem, p.dval)
                        else:
                            need(esem[p.eng], p.sigval)
                    if op.dsem is not None:
                        need(op.dsem.sem, op.dprev)
                        op.emit(eng, op.dsem.sem)
                    else:
                        ins = op.emit(eng)
                        if op.signal:
                            ins.then_inc(esem[engname], 1)
                if engname == SP:
                    for d in final_waits:
                        need(d.sem, d.count * d.unit)

            @block.sync
            def _(e):
                run(SP, e)

            @block.scalar
            def _(e):
                run(ACT, e)

            @block.vector
            def _(e):
                run(DVE, e)

            @block.gpsimd
            def _(e):
                run(POOL, e)

            @block.tensor
            def _(e):
                run(PE, e)


class Arena:
    def __init__(self, nc, es, nbytes):
        self.n = nbytes
        self.t = es.enter_context(nc.sbuf_tensor("arena", [128, nbytes // 2], BF16))
        self.off = 0
        self.peak = 0

    def alloc(self, shape, dtype):
        esz = 4 if dtype == F32 else 2
        n = int(np.prod(shape))
        nb = (n * esz + 31) // 32 * 32
        assert self.off + nb <= self.n, "SBUF arena overflow: need %d have %d" % (self.off + nb, self.n)
        v = self.t[:, self.off // 2:(self.off + n * esz) // 2]
        self.off += nb
        self.peak = max(self.peak, self.off)
        if dtype == F32:
            v = v.bitcast(F32)
        if len(shape) == 2:
            v = v.rearrange("p (a b) -> p a b", b=shape[1])
        elif len(shape) == 3:
            v = v.rearrange("p (a b c) -> p a b c", b=shape[1], c=shape[2])
        return v

    def mark(self):
        return self.off

    def release(self, m):
        self.off = m


def make_cfg(D=4096, S=2048, B=4, MEM_LEN=256, HG=4):
    c = dict(D=D, S=S, B=B, M=MEM_LEN, HG=HG)
    c["T"] = S // 2
    c["DI"] = 2 * D
    c["DMB"] = c["DI"] // 4
    c["DMIX"] = c["DI"] - c["DMB"]
    c["MHD"] = c["DMB"] // 4
    c["PG"] = c["DMIX"] // 4
    c["H"] = c["DMIX"] // 256
    c["IN0"] = c["DMIX"] + c["DMB"] + c["DI"]
    c["IN1"] = 3 * c["DMIX"] + c["DMB"] + c["DI"]
    c["KC"] = D // 128
    c["CI"] = c["DI"] // 128
    c["CM"] = c["DMIX"] // 128
    c["CMB"] = c["DMB"] // 128
    c["MHC"] = c["MHD"] // 128
    c["PGC"] = c["PG"] // 128
    c["NT"] = c["T"] // 512
    c["NKB"] = S // 128
    c["NKO"] = c["T"] // 128
    c["NPIECE"] = c["H"] // HG
    assert c["H"] % HG == 0 and c["T"] % 512 == 0 and c["MHD"] % 128 == 0 and c["PG"] % 128 == 0
    return c


CFG_FULL = make_cfg()
LAM_INIT = 0.8 - 0.6 * math.exp(-0.3 * 1)


def alibi_slopes(n):
    def pow2(m):
        start = 2.0 ** (-8.0 / m)
        return [start ** (i + 1) for i in range(m)]
    if math.log2(n).is_integer():
        s = pow2(n)
    else:
        c = 2 ** math.floor(math.log2(n))
        s = pow2(c) + pow2(2 * c)[0::2][: n - c]
    return np.asarray(s, dtype=np.float32)


def attn_tiles(cfg):
    out = []
    for qc in range(cfg["NT"]):
        for j in range(cfg["NKO"] + 4 * (qc + 1)):
            out.append((qc, j))
    return out


def cvec_layout(cfg):
    KC, CM, H = cfg["KC"], cfg["CM"], cfg["H"]
    o = {}
    off = 0
    for name, n in (("g0", KC), ("gm0", KC), ("g1", KC), ("gm1", KC), ("gf", KC), ("pscale", CM),
                    ("subg", 2), ("cA", H), ("bcol", H * len(attn_tiles(cfg))), ("icnt", 4 * HALO),
                    ("tfull", 640), ("lamv", 4 * 128)):
        o[name] = (off, n)
        off += n
    o["_n"] = off
    return o


class Builder:
    def __init__(self, cfg, debug=False):
        self.cfg = cfg
        self.debug = debug
        self.nc = bass.Bass("TRN2", target_bir_lowering=False)
        self.P = Prog(self.nc)

    def mm(self, out, lhsT, rhs, start, stop, reads, writes):
        self.P.add(PE, lambda e: e.matmul(out, lhsT=lhsT, rhs=rhs, start=start, stop=stop),
                   reads=reads, writes=writes)

    def dma(self, out, in_, reads, writes, dsem, eng=SP):
        self.P.add(eng, lambda e, s: e.dma_start(out=out, in_=in_).then_inc(s, 16),
                   reads=reads, writes=writes, dsem=dsem)

    def act(self, out, in_, func, reads, writes, bias=None, scale=None):
        kw = {}
        if bias is not None:
            kw["bias"] = bias
        if scale is not None:
            kw["scale"] = scale
        self.P.add(ACT, lambda e: e.activation(out, in_, func, **kw), reads=reads, writes=writes)

    def ts(self, out, in0, s1, s2, op0, op1, reads, writes, eng=DVE):
        if op1 is None:
            self.P.add(eng, lambda e: e.tensor_scalar(out, in0, s1, None, op0), reads=reads, writes=writes)
        else:
            self.P.add(eng, lambda e: e.tensor_scalar(out, in0, s1, s2, op0, op1), reads=reads, writes=writes)

    def tt(self, out, in0, in1, op, reads, writes, eng=DVE):
        self.P.add(eng, lambda e: e.tensor_tensor(out, in0, in1, op), reads=reads, writes=writes)

    def stt(self, out, in0, scalar, in1, op0, op1, reads, writes, eng=DVE):
        self.P.add(eng, lambda e: e.scalar_tensor_tensor(out, in0, scalar, in1, op0, op1),
                   reads=reads, writes=writes)

    def dump(self, name, ap, res):
        if not self.debug:
            return
        d = self.nc.dram_tensor("dbg_" + name, list(ap.shape), ap.dtype, kind="ExternalOutput").ap()
        self.dma(d, ap, [res], [], self.next_dsem())

    def next_dsem(self):
        d = self.dpool[self.dpool_i % len(self.dpool)]
        self.dpool_i += 1
        return d

    def wload(self, wd, nk, col0, ncols, row0=0):
        i = self.wslot_i % len(self.wslots)
        self.wslot_i += 1
        assert nk * ncols <= self.slot_elems
        view = self.wslots[i][:, 0:nk * ncols].rearrange("p (k c) -> p k c", c=ncols)
        src = wd[row0:row0 + nk * 128, col0:col0 + ncols].rearrange("(k p) c -> p k c", p=128)
        nsplit = 4 if nk >= 8 else (2 if nk >= 2 else 1)
        bounds = [nk * q // nsplit for q in range(nsplit + 1)]

        def emit(e, s):
            for q in range(nsplit):
                a, b = bounds[q], bounds[q + 1]
                e.dma_start(out=view[:, a:b, :], in_=src[:, a:b, :]).then_inc(s, 16)

        self.P.add(POOL, emit, writes=[self.r_wslots[i]], dsem=self.d_wslots[i], ndma=nsplit)
        return view, self.r_wslots[i]

    def wget(self, cache, key, wd, nk, col0, ncols, row0=0):
        ent = cache.get("e")
        if ent is not None and ent[0] == key and ent[2].w is ent[3]:
            return ent[1], ent[2]
        view, res = self.wload(wd, nk, col0, ncols, row0=row0)
        cache["e"] = (key, view, res, res.w)
        return view, res

    def build(self):
        cfg, nc, P = self.cfg, self.nc, self.P
        D, T, KC, NT, M = cfg["D"], cfg["T"], cfg["KC"], cfg["NT"], cfg["M"]
        CI, CM, CMB, MHC, PGC, H, HG = cfg["CI"], cfg["CM"], cfg["CMB"], cfg["MHC"], cfg["PGC"], cfg["H"], cfg["HG"]
        DMIX, DMB, DI, PG, MHD = cfg["DMIX"], cfg["DMB"], cfg["DI"], cfg["PG"], cfg["MHD"]
        NKB, NKO, NPIECE = cfg["NKB"], cfg["NKO"], cfg["NPIECE"]
        W0 = HALO + T
        lay = cvec_layout(cfg)
        self.lay = lay

        def din(name, shape, dt=F32):
            return nc.dram_tensor(name, list(shape), dt, kind="ExternalInput").ap()

        def dint(name, shape, dt):
            return nc.dram_tensor(name, list(shape), dt)

        xT_d = din("xT", [D, W0])
        memT_d = din("memT", [D, M])
        cvec_d = din("cvec", [128, lay["_n"]])
        w_in0 = din("l0_w_in", [D, cfg["IN0"]])
        w_pool = din("l0_pool_w", [4 * PG, PG])
        w_kv0 = din("l0_w_mem_kv", [D, DMB])
        w_out0 = din("l0_w_out", [DI, D])
        w_in1 = din("l1_w_in", [D, cfg["IN1"]])
        w_kv1 = din("l1_w_mem_kv", [D, DMB])
        w_out1 = din("l1_w_out", [DI, D])
        out_d = nc.dram_tensor("outT", [D, T], F32, kind="ExternalOutput").ap()

        kind_dbg = "ExternalOutput" if self.debug else "Internal"
        yg_d = nc.dram_tensor("ygT", [DI, T], BF16, kind=kind_dbg).ap()
        x1_d = nc.dram_tensor("x1T", [D, T], F32, kind=kind_dbg).ap()
        x2_d = nc.dram_tensor("x2T", [D, T], F32).ap()
        kmine = [dint("kmine%d" % i, [HG * 256, T], BF16) for i in range(NPIECE)]
        kgath = [dint("kgath%d" % i, [2 * HG * 256, T], BF16) for i in range(NPIECE)]
        vmine = [dint("vmine%d" % i, [T, HG * 256], BF16) for i in range(NPIECE)]
        vgath = [dint("vgath%d" % i, [2 * T, HG * 256], BF16) for i in range(NPIECE)]
        HK = DMB // 2
        km_mine = [dint("km_mine%d" % l, [HK, M], BF16) for l in range(2)]
        km_gath = [dint("km_gath%d" % l, [2 * HK, M], BF16) for l in range(2)]
        vm_mine = [dint("vm_mine%d" % l, [M, HK], BF16) for l in range(2)]
        vm_gath = [dint("vm_gath%d" % l, [2 * M, HK], BF16) for l in range(2)]
        r_yg = [Res("yg%d" % j) for j in range(CI)]
        r_x1 = [Res() for _ in range(KC)]
        r_x2 = [Res() for _ in range(KC)]
        r_kmine = [Res() for _ in range(NPIECE)]
        r_vmine = [Res() for _ in range(NPIECE)]
        r_kgath = [Res() for _ in range(NPIECE)]
        r_vgath = [Res() for _ in range(NPIECE)]
        r_const = Res("const")

        with ExitStack() as es:
            ar = Arena(nc, es, 212736)
            self.ar = ar
            ps = [es.enter_context(nc.psum_tensor("ps%d" % i, [128, 512], F32)) for i in range(8)]
            r_ps = [Res("ps%d" % i) for i in range(8)]

            self.slot_elems = max(KC * 256, CI * 128)
            self.wslots = [ar.alloc((self.slot_elems,), BF16) for _ in range(3)]
            self.r_wslots = [Res("ws%d" % i) for i in range(3)]
            self.d_wslots = [P.dsem("ws%d" % i) for i in range(3)]
            self.wslot_i = 0
            self.dpool = [P.dsem("a%d" % i) for i in range(12)]
            self.dpool_i = 0
            d_out = P.dsem("out")
            d_cc = P.dsem("cc", unit=1)

            ones = ar.alloc((128,), BF16)
            cvec = ar.alloc((lay["_n"],), F32)
            rstd = ar.alloc((W0,), F32)
            r_rstd = Res("rstd")
            misc = ar.alloc((16,), F32)
            r_misc = Res("misc")

            def cv(name, i=0, n=1):
                o = lay[name][0] + i
                return cvec[:, o:o + n]

            pair_groups = [[2 * i, 2 * i + 1] for i in range(4)]
            P.add(DVE, lambda e: e.memset(ones, 1.0), writes=[r_const])
            self.dma(cvec, cvec_d, [], [r_const], self.next_dsem())
            m0 = ar.mark()
            ltmp = ar.alloc((256,), F32)
            r_l = Res()
            lv = lay["lamv"][0]
            self.tt(ltmp[:, 0:128], cvec[:, lv:lv + 128], cvec[:, lv + 128:lv + 256], ALU.mult, [r_const], [r_l])
            self.tt(ltmp[:, 128:256], cvec[:, lv + 256:lv + 384], cvec[:, lv + 384:lv + 512], ALU.mult, [r_const, r_l], [r_l])
            P.add(DVE, lambda e: e.reduce_sum(misc[:, 4:5], ltmp[:, 0:128], AX.X), reads=[r_l], writes=[r_misc])
            P.add(DVE, lambda e: e.reduce_sum(misc[:, 5:6], ltmp[:, 128:256], AX.X), reads=[r_l, r_misc], writes=[r_misc])
            self.act(misc[:, 4:6], misc[:, 4:6], AF.Exp, [r_misc], [r_misc])
            self.tt(misc[:, 1:2], misc[:, 4:5], misc[:, 5:6], ALU.subtract, [r_misc], [r_misc])
            self.ts(misc[:, 0:1], misc[:, 1:2], float(LAM_INIT), None, ALU.add, None, [r_misc], [r_misc])
            self.ts(misc[:, 2:4], cv("subg", 0, 2), float(1.0 - LAM_INIT), None, ALU.mult, None, [r_const, r_misc], [r_misc])
            ar.release(m0)
            lam_col = misc[:, 0:1]

            mark_stage = ar.mark()

            def pieces_of(W):
                out = []
                a = 0
                while a < W:
                    b = min(a + 512, W)
                    out.append((a, b))
                    a = b
                return out

            def norm_stage(src_d, src_res, W, gname, hT, r_hT, ss_banks, have_ss):
                m = ar.mark()
                NB = 4
                xb = [ar.alloc((W,), F32) for _ in range(NB)]
                r_xb = [Res() for _ in range(NB)]
                d_xb = [P.dsem("xb%d" % i) for i in range(NB)]
                pcs = pieces_of(W)
                if not have_ss:
                    sqb = [ar.alloc((W,), BF16) for _ in range(NB)]
                    r_sq = [Res() for _ in range(NB)]
                    for c in range(KC):
                        b = c % NB
                        self.dma(xb[b], src_d[c * 128:(c + 1) * 128, :], [src_res[c]] if src_res else [], [r_xb[b]], d_xb[b])
                        self.act(sqb[b], xb[b], AF.Square, [r_xb[b]], [r_sq[b]])
                        for pi, (a, e_) in enumerate(pcs):
                            bk = ss_banks[pi]
                            self.mm(ps[bk][:, 0:e_ - a], ones, sqb[b][:, a:e_], c == 0, c == KC - 1,
                                    [r_sq[b], r_const], [r_ps[bk]])
                    finish_rstd(pcs, ss_banks)
                for c in range(KC):
                    b = c % NB
                    self.dma(xb[b], src_d[c * 128:(c + 1) * 128, :], [src_res[c]] if src_res else [], [r_xb[b]], d_xb[b])
                    self.stt(hT[:, c, :], xb[b], cv(gname, c), rstd[:, 0:W], ALU.mult, ALU.mult,
                             [r_xb[b], r_rstd, r_const], [r_hT])
                ar.release(m)

            def finish_rstd(pcs, ss_banks):
                for pi, (a, e_) in enumerate(pcs):
                    bk = ss_banks[pi]
                    self.act(rstd[:, a:e_], ps[bk][:, 0:e_ - a], AF.Sqrt, [r_ps[bk]], [r_rstd],
                             bias=float(RMS_EPS), scale=1.0 / D)
                W = pcs[-1][1]
                P.add(DVE, lambda e: e.reciprocal(rstd[:, 0:W], rstd[:, 0:W]), reads=[r_rstd], writes=[r_rstd])

            zstate = {}

            def z_silu(w_in, zcol0, j, hT, r_hT, hoff, sz, r_sz, banks):
                g, c = j // 2, j % 2
                wv, r_w = self.wget(zstate, (id(w_in), g), w_in, KC, zcol0 + g * 256, 256)
                for k in range(KC):
                    for n in range(NT):
                        self.mm(ps[banks[n]][:, :], wv[:, k, c * 128:(c + 1) * 128],
                                hT[:, k, hoff + n * 512: hoff + (n + 1) * 512], k == 0, k == KC - 1,
                                [r_w, r_hT], [r_ps[banks[n]]])
                for n in range(NT):
                    self.act(sz[:, n * 512:(n + 1) * 512], ps[banks[n]][:, :], AF.Silu, [r_ps[banks[n]]], [r_sz])

            def emit_yg(j, ygb, r_ygb):
                self.dma(yg_d[j * 128:(j + 1) * 128, :], ygb, [r_ygb], [r_yg[j]], self.next_dsem())

            def memkv_items(w_kv, gname, hmT, r_hm, kmT, r_kmT, vm, r_vm):
                P.barrier()
                norm_stage(memT_d, None, M, gname, hmT, r_hm, [6], False)
                NS = M // 128
                cpg = 256
                items = []

                def k_item(g):
                    wv, r_w = self.wload(w_kv, KC, g * cpg, cpg)
                    for c in range(cpg // 128):
                        for k in range(KC):
                            self.mm(ps[6][:, c * M:(c + 1) * M], wv[:, k, c * 128:(c + 1) * 128], hmT[:, k, :], k == 0, k == KC - 1,
                                    [r_w, r_hm], [r_ps[6]])
                    for c in range(cpg // 128):
                        jj = g * (cpg // 128) + c
                        self.act(kmT[:, jj, :], ps[6][:, c * M:(c + 1) * M], AF.Copy, [r_ps[6]], [r_kmT])

                def v_item(g):
                    wv, r_w = self.wload(w_kv, KC, HK + g * cpg, cpg)
                    for s_ in range(NS):
                        for k in range(KC):
                            self.mm(ps[7][:, s_ * cpg:(s_ + 1) * cpg], hmT[:, k, s_ * 128:(s_ + 1) * 128], wv[:, k, :], k == 0, k == KC - 1,
                                    [r_w, r_hm], [r_ps[7]])
                    for s_ in range(NS):
                        self.act(vm[:, s_, g * cpg:(g + 1) * cpg], ps[7][:, s_ * cpg:(s_ + 1) * cpg], AF.Copy, [r_ps[7]], [r_vm])

                for g in range(HK // cpg):
                    items.append(lambda g=g: k_item(g))
                for g in range(HK // cpg):
                    items.append(lambda g=g: v_item(g))
                return items

            def memkv_exchange(l, kmT, r_kmT, vm, r_vm):
                NS = M // 128
                hc = HK // 128
                r_a, r_b, r_c, r_d = Res(), Res(), Res(), Res()
                self.dma(km_mine[l][:, :].rearrange("(c p) m -> p c m", p=128), kmT[:, 0:hc, :], [r_kmT], [r_a], self.next_dsem())
                self.dma(vm_mine[l][:, :].rearrange("(s p) c -> p s c", p=128), vm[:, :, 0:HK], [r_vm], [r_b], self.next_dsem())
                P.add(POOL, lambda e, s_: e.collective_compute(
                    "AllGather", ALU.bypass, replica_groups=pair_groups,
                    ins=[km_mine[l].ap().opt()], outs=[km_gath[l].ap().opt()]).then_inc(s_, 1),
                    reads=[r_a], writes=[r_c], dsem=P.dsem("cmk%d" % l, unit=1))
                P.add(POOL, lambda e, s_: e.collective_compute(
                    "AllGather", ALU.bypass, replica_groups=pair_groups,
                    ins=[vm_mine[l].ap().opt()], outs=[vm_gath[l].ap().opt()]).then_inc(s_, 1),
                    reads=[r_b], writes=[r_d], dsem=P.dsem("cmv%d" % l, unit=1))
                return r_c, r_d

            def memkv_reload(l, r_c, r_d, kmT, r_kmT, vm, r_vm):
                self.dma(kmT[:, :, :], km_gath[l][:, :].rearrange("(c p) m -> p c m", p=128), [r_c], [r_kmT], self.next_dsem())
                for r in range(2):
                    self.dma(vm[:, :, r * HK:(r + 1) * HK], vm_gath[l][r * M:(r + 1) * M, :].rearrange("(s p) c -> p s c", p=128),
                             [r_d], [r_vm], self.next_dsem())

            def mem_branch(w_in, qcol0, zcol0, hT, r_hT, hoff, kmT, r_kmT, vm, r_vm, sz, r_sz, ygb, r_ygb):
                m = ar.mark()
                NS = M // 128
                qmT = ar.alloc((MHC, T), BF16)
                r_qm = Res()
                eT = ar.alloc((NS, T), BF16)
                r_e = Res()
                rs = ar.alloc((T,), F32)
                r_rs = Res()
                tmp = ar.alloc((T,), F32)
                r_tmp = Res()
                scale = float(MHD ** -0.5)
                gi = 0
                qstate = {}
                for mh in range(4):
                    for dq in range(MHC):
                        jq = mh * MHC + dq
                        wv, r_w = self.wget(qstate, (id(w_in), jq // 2), w_in, KC, qcol0 + (jq // 2) * 256, 256)
                        c = jq % 2
                        banks = (0, 1) if dq % 2 == 0 else (2, 3)
                        for k in range(KC):
                            for n in range(NT):
                                self.mm(ps[banks[n]][:, :], wv[:, k, c * 128:(c + 1) * 128],
                                        hT[:, k, hoff + n * 512: hoff + (n + 1) * 512], k == 0, k == KC - 1,
                                        [r_w, r_hT], [r_ps[banks[n]]])
                        for n in range(NT):
                            self.act(qmT[:, dq, n * 512:(n + 1) * 512], ps[banks[n]][:, :], AF.Copy,
                                     [r_ps[banks[n]]], [r_qm])
                    for n in range(NT):
                        for s in range(NS):
                            bk = 4 + (n * NS + s) % 2
                            for dq in range(MHC):
                                self.mm(ps[bk][:, :], kmT[:, mh * MHC + dq, s * 128:(s + 1) * 128],
                                        qmT[:, dq, n * 512:(n + 1) * 512], dq == 0, dq == MHC - 1,
                                        [r_kmT, r_qm], [r_ps[bk]])
                            self.act(eT[:, s, n * 512:(n + 1) * 512], ps[bk][:, :], AF.Exp, [r_ps[bk]], [r_e], scale=scale)
                    for do in range(MHC):
                        j = CM + mh * MHC + do
                        b = gi % 2
                        gi += 1
                        z_silu(w_in, zcol0, j, hT, r_hT, hoff, sz[b], r_sz[b], (0, 1) if b == 0 else (2, 3))
                        if do == 0:
                            for n in range(NT):
                                for s in range(NS):
                                    self.mm(ps[6][:, :], ones, eT[:, s, n * 512:(n + 1) * 512], s == 0, s == NS - 1,
                                            [r_e, r_const], [r_ps[6]])
                                P.add(DVE, lambda e, n=n: e.reciprocal(rs[:, n * 512:(n + 1) * 512], ps[6][:, :]),
                                      reads=[r_ps[6]], writes=[r_rs])
                        for n in range(NT):
                            for s in range(NS):
                                self.mm(ps[7][:, :], vm[:, s, mh * MHD + do * 128: mh * MHD + (do + 1) * 128],
                                        eT[:, s, n * 512:(n + 1) * 512], s == 0, s == NS - 1,
                                        [r_vm, r_e], [r_ps[7]])
                            self.tt(tmp[:, n * 512:(n + 1) * 512], ps[7][:, :], rs[:, n * 512:(n + 1) * 512], ALU.mult,
                                    [r_ps[7], r_rs], [r_tmp])
                        self.tt(ygb[b], tmp, sz[b], ALU.mult, [r_tmp, r_sz[b]], [r_ygb[b]])
                        emit_yg(j, ygb[b], r_ygb[b])
                ar.release(m)

            def outproj_stage(w_out, xsrc, xsrc_res, xdst, xdst_res):
                m = ar.mark()
                ygres = ar.alloc((CI, T), BF16)
                nsp = 8 if CI % 8 == 0 else 4
                r_ygq = [Res() for _ in range(nsp)]
                qsz = CI // nsp
                for q in range(nsp):
                    a, b = q * qsz, (q + 1) * qsz
                    self.dma(ygres[:, a:b, :], yg_d[a * 128:b * 128, :].rearrange("(c p) t -> p c t", p=128),
                             r_yg[a:b], [r_ygq[q]], self.next_dsem())
                if w_out is w_out0:
                    for q in range(nsp):
                        self.dump("ygres0_%d" % q, ygres[:, q * qsz:(q + 1) * qsz, :], r_ygq[q])
                xc = [ar.alloc((T,), F32) for _ in range(2)]
                r_xc = [Res() for _ in range(2)]
                d_xc = [P.dsem("xc0"), P.dsem("xc1")]
                xo = [ar.alloc((T,), F32) for _ in range(2)]
                r_xo = [Res() for _ in range(2)]
                sq = [ar.alloc((T,), BF16)] * 2
                r_sq = [Res()] * 2
                ssb = [6, 7][:NT]
                KH = CI // 2
                pend_ss = []
                for cp in range(KC // 2):
                    for kh in range(2):
                        wv, r_w = self.wload(w_out, KH, cp * 256, 256, row0=kh * KH * 128)
                        for c2 in range(2):
                            c = cp * 2 + c2
                            banks = (0, 1) if c2 == 0 else (2, 3)
                            if kh == 0:
                                self.dma(xc[c2], xsrc(c), [xsrc_res[c]] if xsrc_res else [], [r_xc[c2]], d_xc[c2])
                            for k in range(KH):
                                kk = kh * KH + k
                                for n in range(NT):
                                    self.mm(ps[banks[n]][:, :], wv[:, k, c2 * 128:(c2 + 1) * 128], ygres[:, kk, n * 512:(n + 1) * 512],
                                            kk == 0, kk == CI - 1, [r_w, r_ygq[kk // qsz]], [r_ps[banks[n]]])
                            if pend_ss:
                                pend_ss.pop(0)()
                            if kh == 1:
                                b = c2
                                for n in range(NT):
                                    self.tt(xo[b][:, n * 512:(n + 1) * 512], ps[banks[n]][:, :], xc[b][:, n * 512:(n + 1) * 512], ALU.add,
                                            [r_ps[banks[n]], r_xc[b]], [r_xo[b]])
                                self.dma(xdst[c * 128:(c + 1) * 128, :], xo[b], [r_xo[b]], [xdst_res[c]], self.next_dsem())
                                self.act(sq[b], xo[b], AF.Square, [r_xo[b]], [r_sq[b]])

                                def ss_mm(c=c, b=b):
                                    for n in range(NT):
                                        self.mm(ps[ssb[n]][:, :], ones, sq[b][:, n * 512:(n + 1) * 512], c == 0, c == KC - 1,
                                                [r_sq[b], r_const], [r_ps[ssb[n]]])
                                pend_ss.append(ss_mm)
                while pend_ss:
                    pend_ss.pop(0)()
                finish_rstd(pieces_of(T), ssb)
                ar.release(m)

            hT = ar.alloc((KC, W0), BF16)
            r_hT = Res("hT")
            sz = [ar.alloc((T,), F32) for _ in range(2)]
            r_sz = [Res() for _ in range(2)]
            ygb = [ar.alloc((T,), BF16) for _ in range(2)]
            r_ygb = [Res() for _ in range(2)]
            kmT = ar.alloc((CMB, M), BF16)
            r_kmT = Res()
            vm = ar.alloc((M // 128, DMB), BF16)
            r_vm = Res()
            hmT = ar.alloc((KC, M), BF16)
            r_hm = Res()
            mark_l = ar.mark()

            norm_stage(xT_d, None, W0, "g0", hT, r_hT, [5, 6, 7], False)
            self.dump("hT0", hT, r_hT)
            self.dump("rstd0", rstd, r_rstd)
            kv_items = memkv_items(w_kv0, "gm0", hmT, r_hm, kmT, r_kmT, vm, r_vm)
            kv_every = max(1, (5 * PGC) // max(1, len(kv_items)))
            kv_step = [0]
            kv_layer = [0]
            kv_done = [False]

            kv_gath = [None]

            def kv_finish():
                if not kv_done[0]:
                    kv_done[0] = True
                    kv_gath[0] = memkv_exchange(kv_layer[0], kmT, r_kmT, vm, r_vm)

            def kv_tick():
                kv_step[0] += 1
                if kv_items and kv_step[0] % kv_every == 0:
                    kv_items.pop(0)()
                    if not kv_items:
                        kv_finish()

            P.barrier()
            pooled = ar.alloc((PGC, T), BF16)
            r_pl = Res()
            uf = ar.alloc((W0,), F32)
            r_uf = Res()
            at = [ar.alloc((W0,), F32) for _ in range(2)]
            r_at = [Res() for _ in range(2)]
            zcol0 = DMIX + DMB
            gi = 0
            ustate = {}
            pwcache = {}
            for g in range(4):
                w = 2 ** (g + 1)
                for cc in range(PGC):
                    jc = g * PGC + cc
                    wv, r_w = self.wget(ustate, jc // 2, w_in0, KC, (jc // 2) * 256, 256)
                    c = jc % 2
                    banks = (0, 1) if cc % 2 == 0 else (2, 3)
                    hb = 4 + cc % 2
                    for k in range(KC):
                        lw = wv[:, k, c * 128:(c + 1) * 128]
                        self.mm(ps[hb][:, 0:HALO], lw, hT[:, k, 0:HALO], k == 0, k == KC - 1, [r_w, r_hT], [r_ps[hb]])
                        for n in range(NT):
                            self.mm(ps[banks[n]][:, :], lw, hT[:, k, HALO + n * 512: HALO + (n + 1) * 512],
                                    k == 0, k == KC - 1, [r_w, r_hT], [r_ps[banks[n]]])
                    self.act(uf[:, 0:HALO], ps[hb][:, 0:HALO], AF.Copy, [r_ps[hb]], [r_uf])
                    for n in range(NT):
                        self.act(uf[:, HALO + n * 512: HALO + (n + 1) * 512], ps[banks[n]][:, :], AF.Copy,
                                 [r_ps[banks[n]]], [r_uf])
                    if jc == 0:
                        self.dump("uf0", uf, r_uf)
                    src, r_src = uf, r_uf
                    sh = 1
                    ai = 0
                    lo = 0
                    while sh < w:
                        lo += sh
                        dst, r_dst = at[ai % 2], r_at[ai % 2]
                        self.tt(dst[:, lo:W0], src[:, lo:W0], src[:, lo - sh:W0 - sh], ALU.add, [r_src], [r_dst])
                        src, r_src = dst, r_dst
                        sh *= 2
                        ai += 1
                    self.stt(pooled[:, cc, :], src[:, HALO:W0], 1.0 / w, uf[:, HALO:W0], ALU.mult, ALU.subtract,
                             [r_src, r_uf], [r_pl])
                    oth, r_oth = at[ai % 2], r_at[ai % 2]
                    self.tt(oth[:, 0:HALO], src[:, HALO:2 * HALO], cv("icnt", g * HALO, HALO), ALU.mult,
                            [r_src, r_const], [r_oth])
                    self.tt(pooled[:, cc, 0:HALO], oth[:, 0:HALO], uf[:, HALO:2 * HALO], ALU.subtract,
                            [r_oth, r_uf, r_pl], [r_pl])
                    kv_tick()
                for dc in range(PGC):
                    j = g * PGC + dc
                    b = gi % 2
                    gi += 1
                    z_silu(w_in0, zcol0, j, hT, r_hT, HALO, sz[b], r_sz[b], (0, 1) if b == 0 else (2, 3))
                    if j == 0:
                        self.dump("sz0", sz[b], r_sz[b])
                        self.dump("pooled0", pooled, r_pl)
                    pc0 = (dc // 2) * 256
                    pwv, r_pw = self.wget(pwcache, (g, dc // 2), w_pool, PGC, pc0, min(256, PG - pc0), row0=g * PG)
                    c = dc % 2
                    mb = (4, 5) if b == 0 else (6, 7)
                    for k in range(PGC):
                        for n in range(NT):
                            self.mm(ps[mb[n]][:, :], pwv[:, k, c * 128:(c + 1) * 128], pooled[:, k, n * 512:(n + 1) * 512],
                                    k == 0, k == PGC - 1, [r_pw, r_pl], [r_ps[mb[n]]])
                    for n in range(NT):
                        self.stt(ygb[b][:, n * 512:(n + 1) * 512], ps[mb[n]][:, :], cv("pscale", j),
                                 sz[b][:, n * 512:(n + 1) * 512], ALU.mult, ALU.mult,
                                 [r_ps[mb[n]], r_sz[b], r_const], [r_ygb[b]])
                    if j == 0:
                        self.dump("ygb0", ygb[b], r_ygb[b])
                    emit_yg(j, ygb[b], r_ygb[b])
                    kv_tick()
            while kv_items:
                kv_items.pop(0)()
            kv_finish()
            ar.release(mark_l)
            P.barrier()
            memkv_reload(0, kv_gath[0][0], kv_gath[0][1], kmT, r_kmT, vm, r_vm)
            mem_branch(w_in0, DMIX, zcol0, hT, r_hT, HALO, kmT, r_kmT, vm, r_vm, sz, r_sz, ygb, r_ygb)

            ar.release(mark_stage)
            P.barrier()
            outproj_stage(w_out0, lambda c: xT_d[c * 128:(c + 1) * 128, HALO:W0], None, x1_d, r_x1)

            ar.release(mark_stage)
            P.barrier()
            hT = ar.alloc((KC, T), BF16)
            r_hT = Res("hT1")
            sz = [ar.alloc((T,), F32) for _ in range(2)]
            r_sz = [Res() for _ in range(2)]
            ygb = [ar.alloc((T,), BF16) for _ in range(2)]
            r_ygb = [Res() for _ in range(2)]
            mark_l = ar.mark()
            kmT = ar.alloc((CMB, M), BF16)
            r_kmT = Res()
            vm = ar.alloc((M // 128, DMB), BF16)
            r_vm = Res()
            hmT = ar.alloc((KC, M), BF16)
            r_hm = Res()
            mark_l2 = ar.mark()

            norm_stage(x1_d, r_x1, T, "g1", hT, r_hT, [6, 7], True)
            kv_items = memkv_items(w_kv1, "gm1", hmT, r_hm, kmT, r_kmT, vm, r_vm)
            kv_every = max(1, (5 * H) // (4 * max(1, len(kv_items))))
            kv_step = [0]
            kv_layer[0] = 1
            kv_done[0] = False

            P.barrier()
            kt = [ar.alloc((T,), BF16) for _ in range(2)]
            r_kt = [Res() for _ in range(2)]
            vt = [ar.alloc((512,), BF16) for _ in range(2)]
            r_vt = [Res() for _ in range(2)]
            groups = [[2 * i, 2 * i + 1] for i in range(4)]

            def exchange(pi):
                P.add(POOL, lambda e, s, pi=pi: e.collective_compute(
                    "AllGather", ALU.bypass, replica_groups=groups,
                    ins=[kmine[pi].ap().opt()], outs=[kgath[pi].ap().opt()]).then_inc(s, 1),
                    reads=[r_kmine[pi]], writes=[r_kgath[pi]], dsem=P.dsem("cck%d" % pi, unit=1))
                P.add(POOL, lambda e, s, pi=pi: e.collective_compute(
                    "AllGather", ALU.bypass, replica_groups=groups,
                    ins=[vmine[pi].ap().opt()], outs=[vgath[pi].ap().opt()]).then_inc(s, 1),
                    reads=[r_vmine[pi]], writes=[r_vgath[pi]], dsem=P.dsem("ccv%d" % pi, unit=1))

            vi = 0
            for pi in range(NPIECE):
                for hl in range(HG):
                    h = pi * HG + hl
                    wv, r_w = self.wload(w_in1, KC, DMIX + h * 256, 256)
                    for mp in range(2):
                        banks = (0, 1) if mp == 0 else (2, 3)
                        for k in range(KC):
                            for n in range(NT):
                                self.mm(ps[banks[n]][:, :], wv[:, k, mp * 128:(mp + 1) * 128], hT[:, k, n * 512:(n + 1) * 512],
                                        k == 0, k == KC - 1, [r_w, r_hT], [r_ps[banks[n]]])
                        for n in range(NT):
                            self.act(kt[mp][:, n * 512:(n + 1) * 512], ps[banks[n]][:, :], AF.Copy, [r_ps[banks[n]]], [r_kt[mp]])
                        row = (hl * 2 + mp) * 128
                        self.dma(kmine[pi][row:row + 128, :], kt[mp], [r_kt[mp]], [r_kmine[pi]], self.next_dsem())
                    kv_tick()
                KH2 = KC // 2
                NTT = T // 128
                assert NTT <= 8 and HG % 2 == 0
                for hp in range(HG // 2):
                    h0 = pi * HG + hp * 2
                    hl0 = hp * 2
                    for kh in range(2):
                        wv, r_w = self.wload(w_in1, KH2, 2 * DMIX + h0 * 256, 512, row0=kh * KH2 * 128)
                        for tt_ in range(NTT):
                            for k in range(KH2):
                                kk = kh * KH2 + k
                                self.mm(ps[tt_][:, :], hT[:, kk, tt_ * 128:(tt_ + 1) * 128], wv[:, k, :], kk == 0, kk == KC - 1,
                                        [r_w, r_hT], [r_ps[tt_]])
                    for tt_ in range(NTT):
                        b = vi % 2
                        vi += 1
                        if b == 0:
                            self.act(vt[b], ps[tt_][:, :], AF.Copy, [r_ps[tt_]], [r_vt[b]])
                        else:
                            P.add(DVE, lambda e, b=b, bk=tt_: e.tensor_copy(vt[b], ps[bk][:, :]), reads=[r_ps[tt_]], writes=[r_vt[b]])
                        self.dma(vmine[pi][tt_ * 128:(tt_ + 1) * 128, hl0 * 256:hl0 * 256 + 512], vt[b], [r_vt[b]],
                                 [r_vmine[pi]], self.next_dsem())
                    kv_tick()
                    kv_tick()
                if pi >= 1:
                    exchange(pi - 1)
            while kv_items:
                kv_items.pop(0)()
            kv_finish()
            ar.release(mark_l2)
            P.barrier()
            zcol1 = 3 * DMIX + DMB
            exchange(NPIECE - 1)
            memkv_reload(1, kv_gath[0][0], kv_gath[0][1], kmT, r_kmT, vm, r_vm)
            mem_branch(w_in1, 3 * DMIX, zcol1, hT, r_hT, 0, kmT, r_kmT, vm, r_vm, sz, r_sz, ygb, r_ygb)

            ar.release(mark_l)
            P.barrier()
            S2 = 2 * T
            kall = [ar.alloc((2, S2), BF16) for _ in range(2)]
            r_kall = [Res() for _ in range(2)]
            vall = [ar.alloc((NKB, 256), BF16) for _ in range(2)]
            r_vall = [Res() for _ in range(2)]
            d_kv = [P.dsem("kv0"), P.dsem("kv1")]
            qT = ar.alloc((2, T), BF16)
            r_qTm = [Res(), Res()]
            NTB, NEB = 4, 6
            tb = [ar.alloc((512,), F32) for _ in range(NTB)]
            r_tb = [Res() for _ in range(NTB)]
            eb = [ar.alloc((512,), BF16) for _ in range(NEB)]
            r_eb = [Res() for _ in range(NEB)]
            r1 = ar.alloc((512,), F32)
            r2 = ar.alloc((512,), F32)
            r_r = Res()
            tbm = ar.alloc((512,), F32)
            r_tab = Res()
            od = ar.alloc((2, 512), F32)
            r_od = Res()
            sqd = ar.alloc((2, 512), BF16)
            r_sqd = Res()
            rr, r_rr = r1, r_r
            yb = ar.alloc((2, T), F32)
            r_yb = Res()
            tiles = attn_tiles(cfg)
            ntile = len(tiles)
            scale = float(128 ** -0.5)
            tf0 = lay["tfull"][0]
            gi = 0
            for h in range(H):
                pi, hl = h // HG, h % HG
                kb_ = h % 2
                def ldkv(e, s, pi=pi, hl=hl, kb_=kb_):
                    for mp in range(2):
                        row = (hl * 2 + mp) * 128
                        e.dma_start(out=kall[kb_][:, mp, 0:T], in_=kgath[pi][row:row + 128, :]).then_inc(s, 16)
                        e.dma_start(out=kall[kb_][:, mp, T:S2], in_=kmine[pi][row:row + 128, :]).then_inc(s, 16)
                    e.dma_start(out=vall[kb_][:, 0:NKO, :],
                                in_=vgath[pi][0:T, hl * 256:(hl + 1) * 256].rearrange("(kb p) c -> p kb c", p=128)).then_inc(s, 16)
                    e.dma_start(out=vall[kb_][:, NKO:NKB, :],
                                in_=vmine[pi][0:T, hl * 256:(hl + 1) * 256].rearrange("(kb p) c -> p kb c", p=128)).then_inc(s, 16)
                P.add(SP, ldkv, reads=[r_kgath[pi], r_vgath[pi], r_kmine[pi], r_vmine[pi]],
                      writes=[r_kall[kb_], r_vall[kb_]], dsem=d_kv[kb_], ndma=6)
                wv, r_w = self.wload(w_in1, KC, h * 256, 256)
                for mp in range(2):
                    banks = (2, 3) if mp == 0 else (4, 5)
                    for k in range(KC):
                        for n in range(NT):
                            self.mm(ps[banks[n]][:, :], wv[:, k, mp * 128:(mp + 1) * 128], hT[:, k, n * 512:(n + 1) * 512],
                                    k == 0, k == KC - 1, [r_w, r_hT], [r_ps[banks[n]]])
                    for n in range(NT):
                        if n == 0:
                            self.act(qT[:, mp, n * 512:(n + 1) * 512], ps[banks[n]][:, :], AF.Copy, [r_ps[banks[n]]], [r_qTm[mp]])
                        else:
                            P.add(DVE, lambda e, mp=mp, n=n, bk=banks[n]: e.tensor_copy(qT[:, mp, n * 512:(n + 1) * 512], ps[bk][:, :]),
                                  reads=[r_ps[banks[n]]], writes=[r_qTm[mp]])
                zb = []
                for qc in range(NT):
                    nkb = NKO + 4 * (qc + 1)
                    tl = [(j, mp) for j in range(nkb) for mp in range(2)]

                    def geom(j):
                        if j < NKO:
                            return 0, 128
                        d_rel = 512 * qc - 128 * (j - NKO)
                        if d_rel >= 128:
                            return 0, 128
                        m_ = -d_rel // 128
                        return 128 * m_, 0

                    def qk(i):
                        j, mp = tl[i]
                        q0, a = geom(j)
                        n_ = 512 - q0
                        sb = mp
                        self.mm(ps[sb][:, 0:n_], kall[kb_][:, mp, j * 128:(j + 1) * 128],
                                qT[:, mp, qc * 512 + q0:(qc + 1) * 512], True, True,
                                [r_kall[kb_], r_qTm[mp]], [r_ps[sb]])
                        t_ = tb[i % NTB]
                        self.stt(t_[:, 0:n_], cvec[:, tf0 + a: tf0 + a + n_], cv("cA", h), ps[sb][:, 0:n_], ALU.mult, ALU.add,
                                 [r_ps[sb], r_const], [r_tb[i % NTB]])
                        tix = h * ntile + tiles.index((qc, j))
                        self.act(eb[i % NEB][:, 0:n_], t_[:, 0:n_], AF.Exp, [r_tb[i % NTB], r_const], [r_eb[i % NEB]],
                                 bias=cv("bcol", tix), scale=scale)

                    def pv(i):
                        j, mp = tl[i]
                        q0, a = geom(j)
                        n_ = 512 - q0
                        first, last = (j == 0), (j == nkb - 1)
                        e_ = eb[i % NEB][:, 0:n_]
                        for dv in range(2):
                            ob = 2 + mp * 2 + dv
                            self.mm(ps[ob][:, q0:512], vall[kb_][:, j, dv * 128:(dv + 1) * 128], e_, first, last,
                                    [r_vall[kb_], r_eb[i % NEB]], [r_ps[ob]])
                        self.mm(ps[6 + mp][:, q0:512], ones, e_, first, last, [r_eb[i % NEB], r_const], [r_ps[6 + mp]])

                    qk(0)
                    qk(1)
                    for i in range(len(tl)):
                        if i + 2 < len(tl):
                            qk(i + 2)
                        pv(i)
                    P.add(DVE, lambda e: e.reciprocal(r1, ps[6][:, :]), reads=[r_ps[6]], writes=[r_r])
                    P.add(DVE, lambda e: e.reciprocal(r2, ps[7][:, :]), reads=[r_ps[7], r_r], writes=[r_r])
                    self.ts(r2, r2, lam_col, None, ALU.mult, None, [r_r, r_misc], [r_r])
                    for dv in range(2):
                        self.tt(od[:, dv, :], ps[2 + dv][:, :], r1, ALU.mult, [r_ps[2 + dv], r_r], [r_od])
                        self.tt(tbm, ps[4 + dv][:, :], r2, ALU.mult, [r_ps[4 + dv], r_r], [r_tab])
                        self.tt(od[:, dv, :], od[:, dv, :], tbm, ALU.subtract, [r_tab, r_od], [r_od])
                    self.act(sqd, od, AF.Square, [r_od], [r_sqd])
                    zdo = [qc] if NT == 2 else [0, 1]
                    for dv in zdo:
                        b = gi % 2
                        gi += 1
                        z_silu(w_in1, zcol1, h * 2 + dv, hT, r_hT, 0, sz[b], r_sz[b], (0, 1))
                        zb.append((dv, b))
                    for dv in range(2):
                        self.mm(ps[0][:, :], ones, sqd[:, dv, :], dv == 0, dv == 1, [r_sqd, r_const], [r_ps[0]])
                    self.act(rr, ps[0][:, :], AF.Ln, [r_ps[0]], [r_rr], bias=float(SUBLN_EPS), scale=1.0 / 256.0)
                    self.act(rr, rr, AF.Exp, [r_rr], [r_rr], scale=-0.5)
                    for dv in range(2):
                        self.stt(yb[:, dv, qc * 512:(qc + 1) * 512], od[:, dv, :], misc[:, 2 + dv:3 + dv], rr, ALU.mult, ALU.mult,
                                 [r_od, r_rr, r_misc], [r_yb])
                for dv, b in zb:
                    j = h * 2 + dv
                    self.tt(ygb[b], yb[:, dv, :], sz[b], ALU.mult, [r_yb, r_sz[b]], [r_ygb[b]])
                    emit_yg(j, ygb[b], r_ygb[b])

            ar.release(mark_stage)
            P.barrier()
            outproj_stage(w_out1, lambda c: x1_d[c * 128:(c + 1) * 128, :], r_x1, x2_d, r_x2)

            ar.release(mark_stage)
            P.barrier()
            NFB = 4
            xb = [ar.alloc((T,), F32) for _ in range(NFB)]
            r_xb = [Res() for _ in range(NFB)]
            d_xb = [P.dsem("fx%d" % i) for i in range(NFB)]
            ob_ = [ar.alloc((T,), F32) for _ in range(NFB)]
            r_ob = [Res() for _ in range(NFB)]
            d_outs = [d_out] + [P.dsem("out%d" % i) for i in range(1, NFB)]
            for c in range(KC):
                b = c % NFB
                self.dma(xb[b], x2_d[c * 128:(c + 1) * 128, :], [r_x2[c]], [r_xb[b]], d_xb[b])
                self.stt(ob_[b], xb[b], cv("gf", c), rstd[:, 0:T], ALU.mult, ALU.mult, [r_xb[b], r_rstd, r_const], [r_ob[b]])
                self.dma(out_d[c * 128:(c + 1) * 128, :], ob_[b], [r_ob[b]], [], d_outs[b])

            finals = d_outs + (self.dpool if self.debug else [])
            P.emit_all(final_waits=finals)
        return nc


def host_tables(cfg, core):
    T, H, NKO = cfg["T"], cfg["H"], cfg["NKO"]
    rank = core % 2
    s0 = rank * T
    slopes = alibi_slopes(H).astype(np.float64)
    scale = 128 ** -0.5
    cA = (-slopes / scale).astype(np.float32)
    tiles = attn_tiles(cfg)
    bcol = np.zeros((H, len(tiles)), np.float32)
    for ti, (qc, j) in enumerate(tiles):
        if j < NKO:
            if rank == 1:
                d = T + 512 * qc - 128 * j
                bcol[:, ti] = -slopes * (d - 128)
            else:
                bcol[:, ti] = -BIG
        else:
            d_rel = 512 * qc - 128 * (j - NKO)
            bcol[:, ti] = -slopes * (d_rel - 128) if d_rel >= 128 else 0.0
    icnt = np.zeros((4, HALO), np.float32)
    for g in range(4):
        w = 2 ** (g + 1)
        for i in range(HALO):
            icnt[g, i] = 1.0 / min(s0 + i + 1, w)
    x = np.arange(640, dtype=np.float64)[None, :]
    k = np.arange(128, dtype=np.float64)[:, None]
    tfull = np.where(x >= k, x - k, BIG).astype(np.float32)
    return cA, bcol.reshape(-1), icnt.reshape(-1), tfull


def fm(v):
    return np.ascontiguousarray(np.asarray(v, np.float32).reshape(-1, 128).T)


def prep_inputs(cfg, inp):
    D, T, S, B = cfg["D"], cfg["T"], cfg["S"], cfg["B"]
    lay = cvec_layout(cfg)
    x = np.asarray(inp["x"], np.float32)
    mem = np.asarray(inp["mem"], np.float32)
    shared = {
        "l0_w_in": np.ascontiguousarray(inp["l0_w_in"], np.float32),
        "l0_pool_w": np.ascontiguousarray(np.asarray(inp["l0_pool_w"], np.float32).reshape(4 * cfg["PG"], cfg["PG"])),
        "l0_w_out": np.ascontiguousarray(inp["l0_w_out"], np.float32),
        "l1_w_in": np.ascontiguousarray(inp["l1_w_in"], np.float32),
        "l1_w_out": np.ascontiguousarray(inp["l1_w_out"], np.float32),
    }
    lamv = np.concatenate([np.asarray(inp[k], np.float32).reshape(-1) for k in
                           ("l1_lambda_q1", "l1_lambda_k1", "l1_lambda_q2", "l1_lambda_k2")])
    maps = []
    for core in range(2 * B):
        b, rank = core // 2, core % 2
        s0 = rank * T
        xe = np.zeros((HALO + T, D), np.float32)
        if s0 > 0:
            xe[:] = x[b, s0 - HALO:s0 + T]
        else:
            xe[HALO:] = x[b, 0:T]
        cA, bcol, icnt, tfull = host_tables(cfg, core)
        cvec = np.zeros((128, lay["_n"]), np.float32)

        def put(name, arr):
            o, n = lay[name]
            arr = np.asarray(arr, np.float32)
            assert arr.shape[-1] == n, (name, arr.shape, n)
            cvec[:, o:o + n] = arr
        put("g0", fm(inp["l0_norm_g"]))
        put("gm0", fm(inp["l0_mem_norm_g"]))
        put("g1", fm(inp["l1_norm_g"]))
        put("gm1", fm(inp["l1_mem_norm_g"]))
        put("gf", fm(inp["final_norm_g"]))
        put("pscale", fm(inp["l0_pool_scale"]))
        put("subg", fm(inp["l1_subln_g"]))
        put("cA", cA[None, :])
        put("bcol", bcol[None, :])
        put("icnt", icnt[None, :])
        put("tfull", tfull)
        put("lamv", lamv[None, :])
        m = {"xT": np.ascontiguousarray(xe.T), "memT": np.ascontiguousarray(mem[b].T), "cvec": cvec}
        DMB_, HK_ = cfg["DMB"], cfg["DMB"] // 2
        for nm in ("l0_w_mem_kv", "l1_w_mem_kv"):
            wkv = np.asarray(inp[nm], np.float32)
            m[nm] = np.ascontiguousarray(np.concatenate(
                [wkv[:, rank * HK_:(rank + 1) * HK_], wkv[:, DMB_ + rank * HK_: DMB_ + (rank + 1) * HK_]], axis=1))
        m.update(shared)
        maps.append(m)
    return maps


def run(cfg, inp, debug=False, trace=False):
    nc = Builder(cfg, debug=debug).build()
    maps = prep_inputs(cfg, inp)
    res = run_bass_kernel_spmd(nc, maps, core_ids=list(range(2 * cfg["B"])), **({"trace": True} if trace else {}))
    T, S, B, D = cfg["T"], cfg["S"], cfg["B"], cfg["D"]
    out = np.empty((B, S, D), np.float32)
    for core in range(2 * B):
        b, rank = core // 2, core % 2
        out[b, rank * T:(rank + 1) * T, :] = res.results[core]["outT"].T
    return out, res


def kernel(**inputs):
    out, _ = run(CFG_FULL, inputs)
    return out
```

```python
import math
from contextlib import ExitStack

import numpy as np
import concourse.bass as bass
import concourse.mybir as mybir
from concourse.bass_utils import run_bass_kernel_spmd

F32 = mybir.dt.float32
BF16 = mybir.dt.bfloat16
ALU = mybir.AluOpType
AF = mybir.ActivationFunctionType
AX = mybir.AxisListType

PE, ACT, DVE, POOL, SP = "tensor", "scalar", "vector", "gpsimd", "sync"
COMPUTE = (PE, ACT, DVE, POOL)
ALL_ENG = (PE, ACT, DVE, POOL, SP)
SAME_ENG_SYNC = {PE: False, ACT: True, DVE: True, POOL: True}

HALO = 16
RMS_EPS = 1e-6
SUBLN_EPS = 1e-5
BIG = 1.0e9


class Res:
    __slots__ = ("name", "w", "r")

    def __init__(self, name=""):
        self.name = name
        self.w = None
        self.r = []


class DSem:
    __slots__ = ("sem", "count", "name", "unit", "last")

    def __init__(self, name, unit=16):
        self.name = name
        self.sem = None
        self.count = 0
        self.unit = unit
        self.last = None


class Op:
    __slots__ = ("eng", "emit", "deps", "signal", "sigval", "dsem", "dprev", "dval")

    def __init__(self, eng, emit):
        self.eng = eng
        self.emit = emit
        self.deps = []
        self.signal = False
        self.sigval = None
        self.dsem = None
        self.dprev = 0
        self.dval = 0


class Prog:
    def __init__(self, nc):
        self.nc = nc
        self.ops = {e: [] for e in ALL_ENG}
        self.dsems = []
        self.pending = {e: [] for e in ALL_ENG}

    def dsem(self, name, unit=16):
        d = DSem(name, unit)
        self.dsems.append(d)
        return d

    def _dep(self, op, p):
        if p is None or p is op:
            return
        if p.dsem is None:
            if p.eng == op.eng and not SAME_ENG_SYNC[p.eng] and op.dsem is None:
                return
            p.signal = True
        op.deps.append(p)

    def add(self, eng, emit, reads=(), writes=(), dsem=None, ndma=1):
        op = Op(eng, emit)
        if dsem is not None:
            op.dsem = dsem
            op.dprev = dsem.count * dsem.unit
            dsem.count += ndma
            op.dval = dsem.count * dsem.unit
            dsem.last = op
        if self.pending[eng]:
            for p in self.pending[eng]:
                self._dep(op, p)
            self.pending[eng] = []
        for r in reads:
            self._dep(op, r.w)
        for w in writes:
            self._dep(op, w.w)
            for q in w.r:
                self._dep(op, q)
        for r in reads:
            if op.dsem is None:
                r.r = [q for q in r.r if not (q.dsem is None and q.eng == eng)]
            r.r.append(op)
        for w in writes:
            w.w = op
            w.r = []
        self.ops[eng].append(op)
        return op

    def barrier(self, engines=(PE, ACT, DVE, SP)):
        tails = []
        for e in engines:
            if self.ops[e]:
                last = None
                for o in reversed(self.ops[e]):
                    if o.dsem is None:
                        last = o
                        break
                if last is not None:
                    tails.append(last)
        for d in self.dsems:
            if d.last is not None and d.last.eng in engines:
                tails.append(d.last)
        for e in engines:
            self.pending[e] = list(tails)

    def emit_all(self, final_waits=()):
        nc = self.nc
        with ExitStack() as es:
            esem = {e: es.enter_context(nc.semaphore("s_" + e)) for e in COMPUTE}
            for i, d in enumerate(self.dsems):
                d.sem = es.enter_context(nc.semaphore("d%d_%s" % (i, d.name)))
            for e in COMPUTE:
                c = 0
                for op in self.ops[e]:
                    if op.signal and op.dsem is None:
                        c += 1
                        op.sigval = c
            block = es.enter_context(nc.Block())
            prog = self

            def run(engname, eng):
                waited = {}

                def need(sem, val):
                    if val <= 0:
                        return
                    k = id(sem)
                    if waited.get(k, 0) >= val:
                        return
                    waited[k] = val
                    eng.wait_ge(sem, val)

                for op in prog.ops[engname]:
                    for p in op.deps:
                        if p.dsem is not None:
                            need(p.dsem.sem, p.dval)
                        else:
                            need(esem[p.eng], p.sigval)
                    if op.dsem is not None:
                        need(op.dsem.sem, op.dprev)
                        op.emit(eng, op.dsem.sem)
                    else:
                        ins = op.emit(eng)
                        if op.signal:
                            ins.then_inc(esem[engname], 1)
                if engname == SP:
                    for d in final_waits:
                        need(d.sem, d.count * d.unit)

            @block.sync
            def _(e):
                run(SP, e)

            @block.scalar
            def _(e):
                run(ACT, e)

            @block.vector
            def _(e):
                run(DVE, e)

            @block.gpsimd
            def _(e):
                run(POOL, e)

            @block.tensor
            def _(e):
                run(PE, e)


class Arena:
    def __init__(self, nc, es, nbytes):
        self.n = nbytes
        self.t = es.enter_context(nc.sbuf_tensor("arena", [128, nbytes // 2], BF16))
        self.off = 0
        self.peak = 0

    def alloc(self, shape, dtype):
        esz = 4 if dtype == F32 else 2
        n = int(np.prod(shape))
        nb = (n * esz + 31) // 32 * 32
        assert self.off + nb <= self.n, "SBUF arena overflow: need %d have %d" % (self.off + nb, self.n)
        v = self.t[:, self.off // 2:(self.off + n * esz) // 2]
        self.off += nb
        self.peak = max(self.peak, self.off)
        if dtype == F32:
            v = v.bitcast(F32)
        if len(shape) == 2:
            v = v.rearrange("p (a b) -> p a b", b=shape[1])
        elif len(shape) == 3:
            v = v.rearrange("p (a b c) -> p a b c", b=shape[1], c=shape[2])
        return v

    def mark(self):
        return self.off

    def release(self, m):
        self.off = m


def make_cfg(D=4096, S=2048, B=4, MEM_LEN=256, HG=4):
    c = dict(D=D, S=S, B=B, M=MEM_LEN, HG=HG)
    c["T"] = S // 2
    c["DI"] = 2 * D
    c["DMB"] = c["DI"] // 4
    c["DMIX"] = c["DI"] - c["DMB"]
    c["MHD"] = c["DMB"] // 4
    c["PG"] = c["DMIX"] // 4
    c["H"] = c["DMIX"] // 256
    c["IN0"] = c["DMIX"] + c["DMB"] + c["DI"]
    c["IN1"] = 3 * c["DMIX"] + c["DMB"] + c["DI"]
    c["KC"] = D // 128
    c["CI"] = c["DI"] // 128
    c["CM"] = c["DMIX"] // 128
    c["CMB"] = c["DMB"] // 128
    c["MHC"] = c["MHD"] // 128
    c["PGC"] = c["PG"] // 128
    c["NT"] = c["T"] // 512
    c["NKB"] = S // 128
    c["NKO"] = c["T"] // 128
    c["NPIECE"] = c["H"] // HG
    assert c["H"] % HG == 0 and c["T"] % 512 == 0 and c["MHD"] % 128 == 0 and c["PG"] % 128 == 0
    return c


CFG_FULL = make_cfg()
LAM_INIT = 0.8 - 0.6 * math.exp(-0.3 * 1)


def alibi_slopes(n):
    def pow2(m):
        start = 2.0 ** (-8.0 / m)
        return [start ** (i + 1) for i in range(m)]
    if math.log2(n).is_integer():
        s = pow2(n)
    else:
        c = 2 ** math.floor(math.log2(n))
        s = pow2(c) + pow2(2 * c)[0::2][: n - c]
    return np.asarray(s, dtype=np.float32)


def attn_tiles(cfg):
    out = []
    for qc in range(cfg["NT"]):
        for j in range(cfg["NKO"] + 4 * (qc + 1)):
            out.append((qc, j))
    return out


def cvec_layout(cfg):
    KC, CM, H = cfg["KC"], cfg["CM"], cfg["H"]
    o = {}
    off = 0
    for name, n in (("g0", KC), ("gm0", KC), ("g1", KC), ("gm1", KC), ("gf", KC), ("pscale", CM),
                    ("subg", 2), ("cA", H), ("bcol", H * len(attn_tiles(cfg))), ("icnt", 4 * HALO),
                    ("tfull", 640), ("lamv", 4 * 128)):
        o[name] = (off, n)
        off += n
    o["_n"] = off
    return o


class Builder:
    def __init__(self, cfg, debug=False):
        self.cfg = cfg
        self.debug = debug
        self.nc = bass.Bass("TRN2", target_bir_lowering=False)
        self.P = Prog(self.nc)

    def mm(self, out, lhsT, rhs, start, stop, reads, writes):
        self.P.add(PE, lambda e: e.matmul(out, lhsT=lhsT, rhs=rhs, start=start, stop=stop),
                   reads=reads, writes=writes)

    def dma(self, out, in_, reads, writes, dsem, eng=SP):
        self.P.add(eng, lambda e, s: e.dma_start(out=out, in_=in_).then_inc(s, 16),
                   reads=reads, writes=writes, dsem=dsem)

    def act(self, out, in_, func, reads, writes, bias=None, scale=None):
        kw = {}
        if bias is not None:
            kw["bias"] = bias
        if scale is not None:
            kw["scale"] = scale
        self.P.add(ACT, lambda e: e.activation(out, in_, func, **kw), reads=reads, writes=writes)

    def ts(self, out, in0, s1, s2, op0, op1, reads, writes, eng=DVE):
        if op1 is None:
            self.P.add(eng, lambda e: e.tensor_scalar(out, in0, s1, None, op0), reads=reads, writes=writes)
        else:
            self.P.add(eng, lambda e: e.tensor_scalar(out, in0, s1, s2, op0, op1), reads=reads, writes=writes)

    def tt(self, out, in0, in1, op, reads, writes, eng=DVE):
        self.P.add(eng, lambda e: e.tensor_tensor(out, in0, in1, op), reads=reads, writes=writes)

    def stt(self, out, in0, scalar, in1, op0, op1, reads, writes, eng=DVE):
        self.P.add(eng, lambda e: e.scalar_tensor_tensor(out, in0, scalar, in1, op0, op1),
                   reads=reads, writes=writes)

    def dump(self, name, ap, res):
        if not self.debug:
            return
        d = self.nc.dram_tensor("dbg_" + name, list(ap.shape), ap.dtype, kind="ExternalOutput").ap()
        self.dma(d, ap, [res], [], self.next_dsem())

    def next_dsem(self):
        d = self.dpool[self.dpool_i % len(self.dpool)]
        self.dpool_i += 1
        return d

    def wload(self, wd, nk, col0, ncols, row0=0):
        i = self.wslot_i % len(self.wslots)
        self.wslot_i += 1
        assert nk * ncols <= self.slot_elems
        view = self.wslots[i][:, 0:nk * ncols].rearrange("p (k c) -> p k c", c=ncols)
        src = wd[row0:row0 + nk * 128, col0:col0 + ncols].rearrange("(k p) c -> p k c", p=128)
        nsplit = 4 if nk >= 8 else (2 if nk >= 2 else 1)
        bounds = [nk * q // nsplit for q in range(nsplit + 1)]

        def emit(e, s):
            for q in range(nsplit):
                a, b = bounds[q], bounds[q + 1]
                e.dma_start(out=view[:, a:b, :], in_=src[:, a:b, :]).then_inc(s, 16)

        self.P.add(POOL, emit, writes=[self.r_wslots[i]], dsem=self.d_wslots[i], ndma=nsplit)
        return view, self.r_wslots[i]

    def wget(self, cache, key, wd, nk, col0, ncols, row0=0):
        ent = cache.get("e")
        if ent is not None and ent[0] == key and ent[2].w is ent[3]:
            return ent[1], ent[2]
        view, res = self.wload(wd, nk, col0, ncols, row0=row0)
        cache["e"] = (key, view, res, res.w)
        return view, res

    def build(self):
        cfg, nc, P = self.cfg, self.nc, self.P
        D, T, KC, NT, M = cfg["D"], cfg["T"], cfg["KC"], cfg["NT"], cfg["M"]
        CI, CM, CMB, MHC, PGC, H, HG = cfg["CI"], cfg["CM"], cfg["CMB"], cfg["MHC"], cfg["PGC"], cfg["H"], cfg["HG"]
        DMIX, DMB, DI, PG, MHD = cfg["DMIX"], cfg["DMB"], cfg["DI"], cfg["PG"], cfg["MHD"]
        NKB, NKO, NPIECE = cfg["NKB"], cfg["NKO"], cfg["NPIECE"]
        W0 = HALO + T
        lay = cvec_layout(cfg)
        self.lay = lay

        def din(name, shape, dt=F32):
            return nc.dram_tensor(name, list(shape), dt, kind="ExternalInput").ap()

        def dint(name, shape, dt):
            return nc.dram_tensor(name, list(shape), dt)

        xT_d = din("xT", [D, W0])
        memT_d = din("memT", [D, M])
        cvec_d = din("cvec", [128, lay["_n"]])
        w_in0 = din("l0_w_in", [D, cfg["IN0"]])
        w_pool = din("l0_pool_w", [4 * PG, PG])
        w_kv0 = din("l0_w_mem_kv", [D, DMB])
        w_out0 = din("l0_w_out", [DI, D])
        w_in1 = din("l1_w_in", [D, cfg["IN1"]])
        w_kv1 = din("l1_w_mem_kv", [D, DMB])
        w_out1 = din("l1_w_out", [DI, D])
        out_d = nc.dram_tensor("outT", [D, T], F32, kind="ExternalOutput").ap()

        kind_dbg = "ExternalOutput" if self.debug else "Internal"
        yg_d = nc.dram_tensor("ygT", [DI, T], BF16, kind=kind_dbg).ap()
        x1_d = nc.dram_tensor("x1T", [D, T], F32, kind=kind_dbg).ap()
        x2_d = nc.dram_tensor("x2T", [D, T], F32).ap()
        kmine = [dint("kmine%d" % i, [HG * 256, T], BF16) for i in range(NPIECE)]
        kgath = [dint("kgath%d" % i, [2 * HG * 256, T], BF16) for i in range(NPIECE)]
        vmine = [dint("vmine%d" % i, [T, HG * 256], BF16) for i in range(NPIECE)]
        vgath = [dint("vgath%d" % i, [2 * T, HG * 256], BF16) for i in range(NPIECE)]
        HK = DMB // 2
        km_mine = [dint("km_mine%d" % l, [HK, M], BF16) for l in range(2)]
        km_gath = [dint("km_gath%d" % l, [2 * HK, M], BF16) for l in range(2)]
        vm_mine = [dint("vm_mine%d" % l, [M, HK], BF16) for l in range(2)]
        vm_gath = [dint("vm_gath%d" % l, [2 * M, HK], BF16) for l in range(2)]
        r_yg = [Res("yg%d" % j) for j in range(CI)]
        r_x1 = [Res() for _ in range(KC)]
        r_x2 = [Res() for _ in range(KC)]
        r_kmine = [Res() for _ in range(NPIECE)]
        r_vmine = [Res() for _ in range(NPIECE)]
        r_kgath = [Res() for _ in range(NPIECE)]
        r_vgath = [Res() for _ in range(NPIECE)]
        r_const = Res("const")

        with ExitStack() as es:
            ar = Arena(nc, es, 212736)
            self.ar = ar
            ps = [es.enter_context(nc.psum_tensor("ps%d" % i, [128, 512], F32)) for i in range(8)]
            r_ps = [Res("ps%d" % i) for i in range(8)]

            self.slot_elems = max(KC * 256, CI * 128)
            self.wslots = [ar.alloc((self.slot_elems,), BF16) for _ in range(3)]
            self.r_wslots = [Res("ws%d" % i) for i in range(3)]
            self.d_wslots = [P.dsem("ws%d" % i) for i in range(3)]
            self.wslot_i = 0
            self.dpool = [P.dsem("a%d" % i) for i in range(12)]
            self.dpool_i = 0
            d_out = P.dsem("out")
            d_cc = P.dsem("cc", unit=1)

            ones = ar.alloc((128,), BF16)
            cvec = ar.alloc((lay["_n"],), F32)
            rstd = ar.alloc((W0,), F32)
            r_rstd = Res("rstd")
            misc = ar.alloc((16,), F32)
            r_misc = Res("misc")

            def cv(name, i=0, n=1):
                o = lay[name][0] + i
                return cvec[:, o:o + n]

            pair_groups = [[2 * i, 2 * i + 1] for i in range(4)]
            P.add(DVE, lambda e: e.memset(ones, 1.0), writes=[r_const])
            self.dma(cvec, cvec_d, [], [r_const], self.next_dsem())
            m0 = ar.mark()
            ltmp = ar.alloc((256,), F32)
            r_l = Res()
            lv = lay["lamv"][0]
            self.tt(ltmp[:, 0:128], cvec[:, lv:lv + 128], cvec[:, lv + 128:lv + 256], ALU.mult, [r_const], [r_l])
            self.tt(ltmp[:, 128:256], cvec[:, lv + 256:lv + 384], cvec[:, lv + 384:lv + 512], ALU.mult, [r_const, r_l], [r_l])
            P.add(DVE, lambda e: e.reduce_sum(misc[:, 4:5], ltmp[:, 0:128], AX.X), reads=[r_l], writes=[r_misc])
            P.add(DVE, lambda e: e.reduce_sum(misc[:, 5:6], ltmp[:, 128:256], AX.X), reads=[r_l, r_misc], writes=[r_misc])
            self.act(misc[:, 4:6], misc[:, 4:6], AF.Exp, [r_misc], [r_misc])
            self.tt(misc[:, 1:2], misc[:, 4:5], misc[:, 5:6], ALU.subtract, [r_misc], [r_misc])
            self.ts(misc[:, 0:1], misc[:, 1:2], float(LAM_INIT), None, ALU.add, None, [r_misc], [r_misc])
            self.ts(misc[:, 2:4], cv("subg", 0, 2), float(1.0 - LAM_INIT), None, ALU.mult, None, [r_const, r_misc], [r_misc])
            ar.release(m0)
            lam_col = misc[:, 0:1]

            mark_stage = ar.mark()

            def pieces_of(W):
                out = []
                a = 0
                while a < W:
                    b = min(a + 512, W)
                    out.append((a, b))
                    a = b
                return out

            def norm_stage(src_d, src_res, W, gname, hT, r_hT, ss_banks, have_ss):
                m = ar.mark()
                NB = 4
                xb = [ar.alloc((W,), F32) for _ in range(NB)]
                r_xb = [Res() for _ in range(NB)]
                d_xb = [P.dsem("xb%d" % i) for i in range(NB)]
                pcs = pieces_of(W)
                if not have_ss:
                    sqb = [ar.alloc((W,), BF16) for _ in range(NB)]
                    r_sq = [Res() for _ in range(NB)]
                    for c in range(KC):
                        b = c % NB
                        self.dma(xb[b], src_d[c * 128:(c + 1) * 128, :], [src_res[c]] if src_res else [], [r_xb[b]], d_xb[b])
                        self.act(sqb[b], xb[b], AF.Square, [r_xb[b]], [r_sq[b]])
                        for pi, (a, e_) in enumerate(pcs):
                            bk = ss_banks[pi]
                            self.mm(ps[bk][:, 0:e_ - a], ones, sqb[b][:, a:e_], c == 0, c == KC - 1,
                                    [r_sq[b], r_const], [r_ps[bk]])
                    finish_rstd(pcs, ss_banks)
                for c in range(KC):
                    b = c % NB
                    self.dma(xb[b], src_d[c * 128:(c + 1) * 128, :], [src_res[c]] if src_res else [], [r_xb[b]], d_xb[b])
                    self.stt(hT[:, c, :], xb[b], cv(gname, c), rstd[:, 0:W], ALU.mult, ALU.mult,
                             [r_xb[b], r_rstd, r_const], [r_hT])
                ar.release(m)

            def finish_rstd(pcs, ss_banks):
                for pi, (a, e_) in enumerate(pcs):
                    bk = ss_banks[pi]
                    self.act(rstd[:, a:e_], ps[bk][:, 0:e_ - a], AF.Sqrt, [r_ps[bk]], [r_rstd],
                             bias=float(RMS_EPS), scale=1.0 / D)
                W = pcs[-1][1]
                P.add(DVE, lambda e: e.reciprocal(rstd[:, 0:W], rstd[:, 0:W]), reads=[r_rstd], writes=[r_rstd])

            zstate = {}

            def z_silu(w_in, zcol0, j, hT, r_hT, hoff, sz, r_sz, banks):
                g, c = j // 2, j % 2
                wv, r_w = self.wget(zstate, (id(w_in), g), w_in, KC, zcol0 + g * 256, 256)
                for k in range(KC):
                    for n in range(NT):
                        self.mm(ps[banks[n]][:, :], wv[:, k, c * 128:(c + 1) * 128],
                                hT[:, k, hoff + n * 512: hoff + (n + 1) * 512], k == 0, k == KC - 1,
                                [r_w, r_hT], [r_ps[banks[n]]])
                for n in range(NT):
                    self.act(sz[:, n * 512:(n + 1) * 512], ps[banks[n]][:, :], AF.Silu, [r_ps[banks[n]]], [r_sz])

            def emit_yg(j, ygb, r_ygb):
                self.dma(yg_d[j * 128:(j + 1) * 128, :], ygb, [r_ygb], [r_yg[j]], self.next_dsem())

            def memkv_items(w_kv, gname, hmT, r_hm, kmT, r_kmT, vm, r_vm):
                P.barrier()
                norm_stage(memT_d, None, M, gname, hmT, r_hm, [6], False)
                NS = M // 128
                cpg = 256
                items = []

                def k_item(g):
                    wv, r_w = self.wload(w_kv, KC, g * cpg, cpg)
                    for c in range(cpg // 128):
                        for k in range(KC):
                            self.mm(ps[6][:, c * M:(c + 1) * M], wv[:, k, c * 128:(c + 1) * 128], hmT[:, k, :], k == 0, k == KC - 1,
                                    [r_w, r_hm], [r_ps[6]])
                    for c in range(cpg // 128):
                        jj = g * (cpg // 128) + c
                        self.act(kmT[:, jj, :], ps[6][:, c * M:(c + 1) * M], AF.Copy, [r_ps[6]], [r_kmT])

                def v_item(g):
                    wv, r_w = self.wload(w_kv, KC, HK + g * cpg, cpg)
                    for s_ in range(NS):
                        for k in range(KC):
                            self.mm(ps[7][:, s_ * cpg:(s_ + 1) * cpg], hmT[:, k, s_ * 128:(s_ + 1) * 128], wv[:, k, :], k == 0, k == KC - 1,
                                    [r_w, r_hm], [r_ps[7]])
                    for s_ in range(NS):
                        self.act(vm[:, s_, g * cpg:(g + 1) * cpg], ps[7][:, s_ * cpg:(s_ + 1) * cpg], AF.Copy, [r_ps[7]], [r_vm])

                for g in range(HK // cpg):
                    items.append(lambda g=g: k_item(g))
                for g in range(HK // cpg):
                    items.append(lambda g=g: v_item(g))
                return items

            def memkv_exchange(l, kmT, r_kmT, vm, r_vm):
                NS = M // 128
                hc = HK // 128
                r_a, r_b, r_c, r_d = Res(), Res(), Res(), Res()
                self.dma(km_mine[l][:, :].rearrange("(c p) m -> p c m", p=128), kmT[:, 0:hc, :], [r_kmT], [r_a], self.next_dsem())
                self.dma(vm_mine[l][:, :].rearrange("(s p) c -> p s c", p=128), vm[:, :, 0:HK], [r_vm], [r_b], self.next_dsem())
                P.add(POOL, lambda e, s_: e.collective_compute(
                    "AllGather", ALU.bypass, replica_groups=pair_groups,
                    ins=[km_mine[l].ap().opt()], outs=[km_gath[l].ap().opt()]).then_inc(s_, 1),
                    reads=[r_a], writes=[r_c], dsem=P.dsem("cmk%d" % l, unit=1))
                P.add(POOL, lambda e, s_: e.collective_compute(
                    "AllGather", ALU.bypass, replica_groups=pair_groups,
                    ins=[vm_mine[l].ap().opt()], outs=[vm_gath[l].ap().opt()]).then_inc(s_, 1),
                    reads=[r_b], writes=[r_d], dsem=P.dsem("cmv%d" % l, unit=1))
                return r_c, r_d

            def memkv_reload(l, r_c, r_d, kmT, r_kmT, vm, r_vm):
                self.dma(kmT[:, :, :], km_gath[l][:, :].rearrange("(c p) m -> p c m", p=128), [r_c], [r_kmT], self.next_dsem())
                for r in range(2):
                    self.dma(vm[:, :, r * HK:(r + 1) * HK], vm_gath[l][r * M:(r + 1) * M, :].rearrange("(s p) c -> p s c", p=128),
                             [r_d], [r_vm], self.next_dsem())

            def mem_branch(w_in, qcol0, zcol0, hT, r_hT, hoff, kmT, r_kmT, vm, r_vm, sz, r_sz, ygb, r_ygb):
                m = ar.mark()
                NS = M // 128
                qmT = ar.alloc((MHC, T), BF16)
                r_qm = Res()
                eT = ar.alloc((NS, T), BF16)
                r_e = Res()
                rs = ar.alloc((T,), F32)
                r_rs = Res()
                tmp = ar.alloc((T,), F32)
                r_tmp = Res()
                scale = float(MHD ** -0.5)
                gi = 0
                qstate = {}
                for mh in range(4):
                    for dq in range(MHC):
                        jq = mh * MHC + dq
                        wv, r_w = self.wget(qstate, (id(w_in), jq // 2), w_in, KC, qcol0 + (jq // 2) * 256, 256)
                        c = jq % 2
                        banks = (0, 1) if dq % 2 == 0 else (2, 3)
                        for k in range(KC):
                            for n in range(NT):
                                self.mm(ps[banks[n]][:, :], wv[:, k, c * 128:(c + 1) * 128],
                                        hT[:, k, hoff + n * 512: hoff + (n + 1) * 512], k == 0, k == KC - 1,
                                        [r_w, r_hT], [r_ps[banks[n]]])
                        for n in range(NT):
                            self.act(qmT[:, dq, n * 512:(n + 1) * 512], ps[banks[n]][:, :], AF.Copy,
                                     [r_ps[banks[n]]], [r_qm])
                    for n in range(NT):
                        for s in range(NS):
                            bk = 4 + (n * NS + s) % 2
                            for dq in range(MHC):
                                self.mm(ps[bk][:, :], kmT[:, mh * MHC + dq, s * 128:(s + 1) * 128],
                                        qmT[:, dq, n * 512:(n + 1) * 512], dq == 0, dq == MHC - 1,
                                        [r_kmT, r_qm], [r_ps[bk]])
                            self.act(eT[:, s, n * 512:(n + 1) * 512], ps[bk][:, :], AF.Exp, [r_ps[bk]], [r_e], scale=scale)
                    for do in range(MHC):
                        j = CM + mh * MHC + do
                        b = gi % 2
                        gi += 1
                        z_silu(w_in, zcol0, j, hT, r_hT, hoff, sz[b], r_sz[b], (0, 1) if b == 0 else (2, 3))
                        if do == 0:
                            for n in range(NT):
                                sbk = 4 + n
                                for s in range(NS):
                                    self.mm(ps[sbk][:, :], ones, eT[:, s, n * 512:(n + 1) * 512], s == 0, s == NS - 1,
                                            [r_e, r_const], [r_ps[sbk]])
                                P.add(DVE, lambda e, n=n, sbk=sbk: e.reciprocal(rs[:, n * 512:(n + 1) * 512], ps[sbk][:, :]),
                                      reads=[r_ps[sbk]], writes=[r_rs])
                        for n in range(NT):
                            obk = 7 - n
                            for s in range(NS):
                                self.mm(ps[obk][:, :], vm[:, s, mh * MHD + do * 128: mh * MHD + (do + 1) * 128],
                                        eT[:, s, n * 512:(n + 1) * 512], s == 0, s == NS - 1,
                                        [r_vm, r_e], [r_ps[obk]])
                            self.tt(tmp[:, n * 512:(n + 1) * 512], ps[obk][:, :], rs[:, n * 512:(n + 1) * 512], ALU.mult,
                                    [r_ps[obk], r_rs], [r_tmp])
                        self.tt(ygb[b], tmp, sz[b], ALU.mult, [r_tmp, r_sz[b]], [r_ygb[b]])
                        emit_yg(j, ygb[b], r_ygb[b])
                ar.release(m)

            def outproj_stage(w_out, xsrc, xsrc_res, xdst, xdst_res):
                m = ar.mark()
                ygres = ar.alloc((CI, T), BF16)
                nsp = 8 if CI % 8 == 0 else 4
                r_ygq = [Res() for _ in range(nsp)]
                qsz = CI // nsp
                for q in range(nsp):
                    a, b = q * qsz, (q + 1) * qsz
                    self.dma(ygres[:, a:b, :], yg_d[a * 128:b * 128, :].rearrange("(c p) t -> p c t", p=128),
                             r_yg[a:b], [r_ygq[q]], self.next_dsem())
                if w_out is w_out0:
                    for q in range(nsp):
                        self.dump("ygres0_%d" % q, ygres[:, q * qsz:(q + 1) * qsz, :], r_ygq[q])
                xc = [ar.alloc((T,), F32) for _ in range(2)]
                r_xc = [Res() for _ in range(2)]
                d_xc = [P.dsem("xc0"), P.dsem("xc1")]
                xo = [ar.alloc((T,), F32) for _ in range(2)]
                r_xo = [Res() for _ in range(2)]
                sq = [ar.alloc((T,), BF16)] * 2
                r_sq = [Res()] * 2
                ssb = [6, 7][:NT]
                KH = CI // 2
                pend_ss = []
                for cp in range(KC // 2):
                    for kh in range(2):
                        wv, r_w = self.wload(w_out, KH, cp * 256, 256, row0=kh * KH * 128)
                        for c2 in range(2):
                            c = cp * 2 + c2
                            banks = (0, 1) if c2 == 0 else (2, 3)
                            if kh == 0:
                                self.dma(xc[c2], xsrc(c), [xsrc_res[c]] if xsrc_res else [], [r_xc[c2]], d_xc[c2])
                            for k in range(KH):
                                kk = kh * KH + k
                                for n in range(NT):
                                    self.mm(ps[banks[n]][:, :], wv[:, k, c2 * 128:(c2 + 1) * 128], ygres[:, kk, n * 512:(n + 1) * 512],
                                            kk == 0, kk == CI - 1, [r_w, r_ygq[kk // qsz]], [r_ps[banks[n]]])
                            if pend_ss:
                                pend_ss.pop(0)()
                            if kh == 1:
                                b = c2
                                for n in range(NT):
                                    self.tt(xo[b][:, n * 512:(n + 1) * 512], ps[banks[n]][:, :], xc[b][:, n * 512:(n + 1) * 512], ALU.add,
                                            [r_ps[banks[n]], r_xc[b]], [r_xo[b]])
                                self.dma(xdst[c * 128:(c + 1) * 128, :], xo[b], [r_xo[b]], [xdst_res[c]], self.next_dsem())
                                self.act(sq[b], xo[b], AF.Square, [r_xo[b]], [r_sq[b]])

                                def ss_mm(c=c, b=b):
                                    for n in range(NT):
                                        self.mm(ps[ssb[n]][:, :], ones, sq[b][:, n * 512:(n + 1) * 512], c == 0, c == KC - 1,
                                                [r_sq[b], r_const], [r_ps[ssb[n]]])
                                pend_ss.append(ss_mm)
                while pend_ss:
                    pend_ss.pop(0)()
                finish_rstd(pieces_of(T), ssb)
                ar.release(m)

            hT = ar.alloc((KC, W0), BF16)
            r_hT = Res("hT")
            sz = [ar.alloc((T,), F32) for _ in range(2)]
            r_sz = [Res() for _ in range(2)]
            ygb = [ar.alloc((T,), BF16) for _ in range(2)]
            r_ygb = [Res() for _ in range(2)]
            kmT = ar.alloc((CMB, M), BF16)
            r_kmT = Res()
            vm = ar.alloc((M // 128, DMB), BF16)
            r_vm = Res()
            hmT = ar.alloc((KC, M), BF16)
            r_hm = Res()
            mark_l = ar.mark()

            norm_stage(xT_d, None, W0, "g0", hT, r_hT, [5, 6, 7], False)
            self.dump("hT0", hT, r_hT)
            self.dump("rstd0", rstd, r_rstd)
            kv_items = memkv_items(w_kv0, "gm0", hmT, r_hm, kmT, r_kmT, vm, r_vm)
            kv_every = max(1, (5 * PGC) // max(1, len(kv_items)))
            kv_step = [0]
            kv_layer = [0]
            kv_done = [False]

            kv_gath = [None]

            def kv_finish():
                if not kv_done[0]:
                    kv_done[0] = True
                    kv_gath[0] = memkv_exchange(kv_layer[0], kmT, r_kmT, vm, r_vm)

            def kv_tick():
                kv_step[0] += 1
                if kv_items and kv_step[0] % kv_every == 0:
                    kv_items.pop(0)()
                    if not kv_items:
                        kv_finish()

            P.barrier()
            pooled = ar.alloc((PGC, T), BF16)
            r_pl = Res()
            uf = ar.alloc((W0,), F32)
            r_uf = Res()
            at = [ar.alloc((W0,), F32) for _ in range(2)]
            r_at = [Res() for _ in range(2)]
            zcol0 = DMIX + DMB
            gi = 0
            ustate = {}
            pwcache = {}
            for g in range(4):
                w = 2 ** (g + 1)
                for cc in range(PGC):
                    jc = g * PGC + cc
                    wv, r_w = self.wget(ustate, jc // 2, w_in0, KC, (jc // 2) * 256, 256)
                    c = jc % 2
                    banks = (0, 1) if cc % 2 == 0 else (2, 3)
                    hb = 4 + cc % 2
                    for k in range(KC):
                        lw = wv[:, k, c * 128:(c + 1) * 128]
                        self.mm(ps[hb][:, 0:HALO], lw, hT[:, k, 0:HALO], k == 0, k == KC - 1, [r_w, r_hT], [r_ps[hb]])
                        for n in range(NT):
                            self.mm(ps[banks[n]][:, :], lw, hT[:, k, HALO + n * 512: HALO + (n + 1) * 512],
                                    k == 0, k == KC - 1, [r_w, r_hT], [r_ps[banks[n]]])
                    self.act(uf[:, 0:HALO], ps[hb][:, 0:HALO], AF.Copy, [r_ps[hb]], [r_uf])
                    for n in range(NT):
                        self.act(uf[:, HALO + n * 512: HALO + (n + 1) * 512], ps[banks[n]][:, :], AF.Copy,
                                 [r_ps[banks[n]]], [r_uf])
                    if jc == 0:
                        self.dump("uf0", uf, r_uf)
                    src, r_src = uf, r_uf
                    sh = 1
                    ai = 0
                    lo = 0
                    while sh < w:
                        lo += sh
                        dst, r_dst = at[ai % 2], r_at[ai % 2]
                        self.tt(dst[:, lo:W0], src[:, lo:W0], src[:, lo - sh:W0 - sh], ALU.add, [r_src], [r_dst])
                        src, r_src = dst, r_dst
                        sh *= 2
                        ai += 1
                    self.stt(pooled[:, cc, :], src[:, HALO:W0], 1.0 / w, uf[:, HALO:W0], ALU.mult, ALU.subtract,
                             [r_src, r_uf], [r_pl])
                    oth, r_oth = at[ai % 2], r_at[ai % 2]
                    self.tt(oth[:, 0:HALO], src[:, HALO:2 * HALO], cv("icnt", g * HALO, HALO), ALU.mult,
                            [r_src, r_const], [r_oth])
                    self.tt(pooled[:, cc, 0:HALO], oth[:, 0:HALO], uf[:, HALO:2 * HALO], ALU.subtract,
                            [r_oth, r_uf, r_pl], [r_pl])
                    kv_tick()
                for dc in range(PGC):
                    j = g * PGC + dc
                    b = gi % 2
                    gi += 1
                    z_silu(w_in0, zcol0, j, hT, r_hT, HALO, sz[b], r_sz[b], (0, 1) if b == 0 else (2, 3))
                    if j == 0:
                        self.dump("sz0", sz[b], r_sz[b])
                        self.dump("pooled0", pooled, r_pl)
                    pc0 = (dc // 2) * 256
                    pwv, r_pw = self.wget(pwcache, (g, dc // 2), w_pool, PGC, pc0, min(256, PG - pc0), row0=g * PG)
                    c = dc % 2
                    mb = (4, 5) if b == 0 else (6, 7)
                    for k in range(PGC):
                        for n in range(NT):
                            self.mm(ps[mb[n]][:, :], pwv[:, k, c * 128:(c + 1) * 128], pooled[:, k, n * 512:(n + 1) * 512],
                                    k == 0, k == PGC - 1, [r_pw, r_pl], [r_ps[mb[n]]])
                    for n in range(NT):
                        self.stt(ygb[b][:, n * 512:(n + 1) * 512], ps[mb[n]][:, :], cv("pscale", j),
                                 sz[b][:, n * 512:(n + 1) * 512], ALU.mult, ALU.mult,
                                 [r_ps[mb[n]], r_sz[b], r_const], [r_ygb[b]])
                    if j == 0:
                        self.dump("ygb0", ygb[b], r_ygb[b])
                    emit_yg(j, ygb[b], r_ygb[b])
                    kv_tick()
            while kv_items:
                kv_items.pop(0)()
            kv_finish()
            ar.release(mark_l)
            P.barrier()
            memkv_reload(0, kv_gath[0][0], kv_gath[0][1], kmT, r_kmT, vm, r_vm)
            mem_branch(w_in0, DMIX, zcol0, hT, r_hT, HALO, kmT, r_kmT, vm, r_vm, sz, r_sz, ygb, r_ygb)

            ar.release(mark_stage)
            P.barrier()
            outproj_stage(w_out0, lambda c: xT_d[c * 128:(c + 1) * 128, HALO:W0], None, x1_d, r_x1)

            ar.release(mark_stage)
            P.barrier()
            hT = ar.alloc((KC, T), BF16)
            r_hT = Res("hT1")
            sz = [ar.alloc((T,), F32) for _ in range(2)]
            r_sz = [Res() for _ in range(2)]
            ygb = [ar.alloc((T,), BF16) for _ in range(2)]
            r_ygb = [Res() for _ in range(2)]
            mark_l = ar.mark()
            kmT = ar.alloc((CMB, M), BF16)
            r_kmT = Res()
            vm = ar.alloc((M // 128, DMB), BF16)
            r_vm = Res()
            hmT = ar.alloc((KC, M), BF16)
            r_hm = Res()
            mark_l2 = ar.mark()

            norm_stage(x1_d, r_x1, T, "g1", hT, r_hT, [6, 7], True)
            kv_items = memkv_items(w_kv1, "gm1", hmT, r_hm, kmT, r_kmT, vm, r_vm)
            kv_every = max(1, (5 * H) // (4 * max(1, len(kv_items))))
            kv_step = [0]
            kv_layer[0] = 1
            kv_done[0] = False

            P.barrier()
            kt = [ar.alloc((T,), BF16) for _ in range(2)]
            r_kt = [Res() for _ in range(2)]
            vt = [ar.alloc((512,), BF16) for _ in range(2)]
            r_vt = [Res() for _ in range(2)]
            groups = [[2 * i, 2 * i + 1] for i in range(4)]

            def exchange(pi):
                P.add(POOL, lambda e, s, pi=pi: e.collective_compute(
                    "AllGather", ALU.bypass, replica_groups=groups,
                    ins=[kmine[pi].ap().opt()], outs=[kgath[pi].ap().opt()]).then_inc(s, 1),
                    reads=[r_kmine[pi]], writes=[r_kgath[pi]], dsem=P.dsem("cck%d" % pi, unit=1))
                P.add(POOL, lambda e, s, pi=pi: e.collective_compute(
                    "AllGather", ALU.bypass, replica_groups=groups,
                    ins=[vmine[pi].ap().opt()], outs=[vgath[pi].ap().opt()]).then_inc(s, 1),
                    reads=[r_vmine[pi]], writes=[r_vgath[pi]], dsem=P.dsem("ccv%d" % pi, unit=1))

            vi = 0
            for pi in range(NPIECE):
                for hl in range(HG):
                    h = pi * HG + hl
                    wv, r_w = self.wload(w_in1, KC, DMIX + h * 256, 256)
                    for mp in range(2):
                        banks = (0, 1) if mp == 0 else (2, 3)
                        for k in range(KC):
                            for n in range(NT):
                                self.mm(ps[banks[n]][:, :], wv[:, k, mp * 128:(mp + 1) * 128], hT[:, k, n * 512:(n + 1) * 512],
                                        k == 0, k == KC - 1, [r_w, r_hT], [r_ps[banks[n]]])
                        for n in range(NT):
                            self.act(kt[mp][:, n * 512:(n + 1) * 512], ps[banks[n]][:, :], AF.Copy, [r_ps[banks[n]]], [r_kt[mp]])
                        row = (hl * 2 + mp) * 128
                        self.dma(kmine[pi][row:row + 128, :], kt[mp], [r_kt[mp]], [r_kmine[pi]], self.next_dsem())
                    kv_tick()
                KH2 = KC // 2
                NTT = T // 128
                assert NTT <= 8 and HG % 2 == 0
                for hp in range(HG // 2):
                    h0 = pi * HG + hp * 2
                    hl0 = hp * 2
                    for kh in range(2):
                        wv, r_w = self.wload(w_in1, KH2, 2 * DMIX + h0 * 256, 512, row0=kh * KH2 * 128)
                        for tt_ in range(NTT):
                            for k in range(KH2):
                                kk = kh * KH2 + k
                                self.mm(ps[tt_][:, :], hT[:, kk, tt_ * 128:(tt_ + 1) * 128], wv[:, k, :], kk == 0, kk == KC - 1,
                                        [r_w, r_hT], [r_ps[tt_]])
                    for tt_ in range(NTT):
                        b = vi % 2
                        vi += 1
                        if b == 0:
                            self.act(vt[b], ps[tt_][:, :], AF.Copy, [r_ps[tt_]], [r_vt[b]])
                        else:
                            P.add(DVE, lambda e, b=b, bk=tt_: e.tensor_copy(vt[b], ps[bk][:, :]), reads=[r_ps[tt_]], writes=[r_vt[b]])
                        self.dma(vmine[pi][tt_ * 128:(tt_ + 1) * 128, hl0 * 256:hl0 * 256 + 512], vt[b], [r_vt[b]],
                                 [r_vmine[pi]], self.next_dsem())
                    kv_tick()
                    kv_tick()
                if pi >= 1:
                    exchange(pi - 1)
            while kv_items:
                kv_items.pop(0)()
            kv_finish()
            ar.release(mark_l2)
            P.barrier()
            zcol1 = 3 * DMIX + DMB
            exchange(NPIECE - 1)
            memkv_reload(1, kv_gath[0][0], kv_gath[0][1], kmT, r_kmT, vm, r_vm)
            mem_branch(w_in1, 3 * DMIX, zcol1, hT, r_hT, 0, kmT, r_kmT, vm, r_vm, sz, r_sz, ygb, r_ygb)

            ar.release(mark_l)
            P.barrier()
            S2 = 2 * T
            kall = [ar.alloc((2, S2), BF16) for _ in range(2)]
            r_kall = [Res() for _ in range(2)]
            vall = [ar.alloc((NKB, 256), BF16) for _ in range(2)]
            r_vall = [Res() for _ in range(2)]
            d_kv = [P.dsem("kv0"), P.dsem("kv1")]
            qT = ar.alloc((2, T), BF16)
            r_qTm = [Res(), Res()]
            NTB, NEB = 4, 6
            tb = [ar.alloc((512,), F32) for _ in range(NTB)]
            r_tb = [Res() for _ in range(NTB)]
            eb = [ar.alloc((512,), BF16) for _ in range(NEB)]
            r_eb = [Res() for _ in range(NEB)]
            r1 = ar.alloc((512,), F32)
            r2 = ar.alloc((512,), F32)
            r_r = Res()
            tbm = ar.alloc((512,), F32)
            r_tab = Res()
            od = ar.alloc((2, 512), F32)
            r_od = Res()
            sqd = ar.alloc((2, 512), BF16)
            r_sqd = Res()
            rr, r_rr = r1, r_r
            yb = ar.alloc((2, T), F32)
            r_yb = Res()
            tiles = attn_tiles(cfg)
            ntile = len(tiles)
            scale = float(128 ** -0.5)
            tf0 = lay["tfull"][0]
            gi = 0
            for h in range(H):
                pi, hl = h // HG, h % HG
                kb_ = h % 2
                def ldkv(e, s, pi=pi, hl=hl, kb_=kb_):
                    for mp in range(2):
                        row = (hl * 2 + mp) * 128
                        e.dma_start(out=kall[kb_][:, mp, 0:T], in_=kgath[pi][row:row + 128, :]).then_inc(s, 16)
                        e.dma_start(out=kall[kb_][:, mp, T:S2], in_=kmine[pi][row:row + 128, :]).then_inc(s, 16)
                    e.dma_start(out=vall[kb_][:, 0:NKO, :],
                                in_=vgath[pi][0:T, hl * 256:(hl + 1) * 256].rearrange("(kb p) c -> p kb c", p=128)).then_inc(s, 16)
                    e.dma_start(out=vall[kb_][:, NKO:NKB, :],
                                in_=vmine[pi][0:T, hl * 256:(hl + 1) * 256].rearrange("(kb p) c -> p kb c", p=128)).then_inc(s, 16)
                P.add(SP, ldkv, reads=[r_kgath[pi], r_vgath[pi], r_kmine[pi], r_vmine[pi]],
                      writes=[r_kall[kb_], r_vall[kb_]], dsem=d_kv[kb_], ndma=6)
                wv, r_w = self.wload(w_in1, KC, h * 256, 256)
                for mp in range(2):
                    banks = (2, 3) if mp == 0 else (4, 5)
                    for k in range(KC):
                        for n in range(NT):
                            self.mm(ps[banks[n]][:, :], wv[:, k, mp * 128:(mp + 1) * 128], hT[:, k, n * 512:(n + 1) * 512],
                                    k == 0, k == KC - 1, [r_w, r_hT], [r_ps[banks[n]]])
                    for n in range(NT):
                        if n == 0:
                            self.act(qT[:, mp, n * 512:(n + 1) * 512], ps[banks[n]][:, :], AF.Copy, [r_ps[banks[n]]], [r_qTm[mp]])
                        else:
                            P.add(DVE, lambda e, mp=mp, n=n, bk=banks[n]: e.tensor_copy(qT[:, mp, n * 512:(n + 1) * 512], ps[bk][:, :]),
                                  reads=[r_ps[banks[n]]], writes=[r_qTm[mp]])
                zb = []
                for qc in range(NT):
                    nkb = NKO + 4 * (qc + 1)
                    tl = [(j, mp) for j in range(nkb) for mp in range(2)]

                    def geom(j):
                        if j < NKO:
                            return 0, 128
                        d_rel = 512 * qc - 128 * (j - NKO)
                        if d_rel >= 128:
                            return 0, 128
                        m_ = -d_rel // 128
                        return 128 * m_, 0

                    def qk(i):
                        j, mp = tl[i]
                        q0, a = geom(j)
                        n_ = 512 - q0
                        sb = mp
                        self.mm(ps[sb][:, 0:n_], kall[kb_][:, mp, j * 128:(j + 1) * 128],
                                qT[:, mp, qc * 512 + q0:(qc + 1) * 512], True, True,
                                [r_kall[kb_], r_qTm[mp]], [r_ps[sb]])
                        t_ = tb[i % NTB]
                        self.stt(t_[:, 0:n_], cvec[:, tf0 + a: tf0 + a + n_], cv("cA", h), ps[sb][:, 0:n_], ALU.mult, ALU.add,
                                 [r_ps[sb], r_const], [r_tb[i % NTB]])
                        tix = h * ntile + tiles.index((qc, j))
                        self.act(eb[i % NEB][:, 0:n_], t_[:, 0:n_], AF.Exp, [r_tb[i % NTB], r_const], [r_eb[i % NEB]],
                                 bias=cv("bcol", tix), scale=scale)

                    def pv(i):
                        j, mp = tl[i]
                        q0, a = geom(j)
                        n_ = 512 - q0
                        first, last = (j == 0), (j == nkb - 1)
                        e_ = eb[i % NEB][:, 0:n_]
                        for dv in range(2):
                            ob = 2 + mp * 2 + dv
                            self.mm(ps[ob][:, q0:512], vall[kb_][:, j, dv * 128:(dv + 1) * 128], e_, first, last,
                                    [r_vall[kb_], r_eb[i % NEB]], [r_ps[ob]])
                        self.mm(ps[6 + mp][:, q0:512], ones, e_, first, last, [r_eb[i % NEB], r_const], [r_ps[6 + mp]])

                    qk(0)
                    qk(1)
                    for i in range(len(tl)):
                        if i + 2 < len(tl):
                            qk(i + 2)
                        pv(i)
                    P.add(DVE, lambda e: e.reciprocal(r1, ps[6][:, :]), reads=[r_ps[6]], writes=[r_r])
                    P.add(DVE, lambda e: e.reciprocal(r2, ps[7][:, :]), reads=[r_ps[7], r_r], writes=[r_r])
                    self.ts(r2, r2, lam_col, None, ALU.mult, None, [r_r, r_misc], [r_r])
                    for dv in range(2):
                        self.tt(od[:, dv, :], ps[2 + dv][:, :], r1, ALU.mult, [r_ps[2 + dv], r_r], [r_od])
                        self.tt(tbm, ps[4 + dv][:, :], r2, ALU.mult, [r_ps[4 + dv], r_r], [r_tab])
                        self.tt(od[:, dv, :], od[:, dv, :], tbm, ALU.subtract, [r_tab, r_od], [r_od])
                    self.act(sqd, od, AF.Square, [r_od], [r_sqd])
                    zdo = [qc] if NT == 2 else [0, 1]
                    for dv in zdo:
                        b = gi % 2
                        gi += 1
                        z_silu(w_in1, zcol1, h * 2 + dv, hT, r_hT, 0, sz[b], r_sz[b], (0, 1))
                        zb.append((dv, b))
                    for dv in range(2):
                        self.mm(ps[0][:, :], ones, sqd[:, dv, :], dv == 0, dv == 1, [r_sqd, r_const], [r_ps[0]])
                    self.act(rr, ps[0][:, :], AF.Ln, [r_ps[0]], [r_rr], bias=float(SUBLN_EPS), scale=1.0 / 256.0)
                    self.act(rr, rr, AF.Exp, [r_rr], [r_rr], scale=-0.5)
                    for dv in range(2):
                        self.stt(yb[:, dv, qc * 512:(qc + 1) * 512], od[:, dv, :], misc[:, 2 + dv:3 + dv], rr, ALU.mult, ALU.mult,
                                 [r_od, r_rr, r_misc], [r_yb])
                for dv, b in zb:
                    j = h * 2 + dv
                    self.tt(ygb[b], yb[:, dv, :], sz[b], ALU.mult, [r_yb, r_sz[b]], [r_ygb[b]])
                    emit_yg(j, ygb[b], r_ygb[b])

            ar.release(mark_stage)
            P.barrier()
            outproj_stage(w_out1, lambda c: x1_d[c * 128:(c + 1) * 128, :], r_x1, x2_d, r_x2)

            ar.release(mark_stage)
            P.barrier()
            NFB = 4
            xb = [ar.alloc((T,), F32) for _ in range(NFB)]
            r_xb = [Res() for _ in range(NFB)]
            d_xb = [P.dsem("fx%d" % i) for i in range(NFB)]
            ob_ = [ar.alloc((T,), F32) for _ in range(NFB)]
            r_ob = [Res() for _ in range(NFB)]
            d_outs = [d_out] + [P.dsem("out%d" % i) for i in range(1, NFB)]
            for c in range(KC):
                b = c % NFB
                self.dma(xb[b], x2_d[c * 128:(c + 1) * 128, :], [r_x2[c]], [r_xb[b]], d_xb[b])
                self.stt(ob_[b], xb[b], cv("gf", c), rstd[:, 0:T], ALU.mult, ALU.mult, [r_xb[b], r_rstd, r_const], [r_ob[b]])
                self.dma(out_d[c * 128:(c + 1) * 128, :], ob_[b], [r_ob[b]], [], d_outs[b])

            finals = d_outs + (self.dpool if self.debug else [])
            P.emit_all(final_waits=finals)
        return nc


def host_tables(cfg, core):
    T, H, NKO = cfg["T"], cfg["H"], cfg["NKO"]
    rank = core % 2
    s0 = rank * T
    slopes = alibi_slopes(H).astype(np.float64)
    scale = 128 ** -0.5
    cA = (-slopes / scale).astype(np.float32)
    tiles = attn_tiles(cfg)
    bcol = np.zeros((H, len(tiles)), np.float32)
    for ti, (qc, j) in enumerate(tiles):
        if j < NKO:
            if rank == 1:
                d = T + 512 * qc - 128 * j
                bcol[:, ti] = -slopes * (d - 128)
            else:
                bcol[:, ti] = -BIG
        else:
            d_rel = 512 * qc - 128 * (j - NKO)
            bcol[:, ti] = -slopes * (d_rel - 128) if d_rel >= 128 else 0.0
    icnt = np.zeros((4, HALO), np.float32)
    for g in range(4):
        w = 2 ** (g + 1)
        for i in range(HALO):
            icnt[g, i] = 1.0 / min(s0 + i + 1, w)
    x = np.arange(640, dtype=np.float64)[None, :]
    k = np.arange(128, dtype=np.float64)[:, None]
    tfull = np.where(x >= k, x - k, BIG).astype(np.float32)
    return cA, bcol.reshape(-1), icnt.reshape(-1), tfull


def fm(v):
    return np.ascontiguousarray(np.asarray(v, np.float32).reshape(-1, 128).T)


def prep_inputs(cfg, inp):
    D, T, S, B = cfg["D"], cfg["T"], cfg["S"], cfg["B"]
    lay = cvec_layout(cfg)
    x = np.asarray(inp["x"], np.float32)
    mem = np.asarray(inp["mem"], np.float32)
    shared = {
        "l0_w_in": np.ascontiguousarray(inp["l0_w_in"], np.float32),
        "l0_pool_w": np.ascontiguousarray(np.asarray(inp["l0_pool_w"], np.float32).reshape(4 * cfg["PG"], cfg["PG"])),
        "l0_w_out": np.ascontiguousarray(inp["l0_w_out"], np.float32),
        "l1_w_in": np.ascontiguousarray(inp["l1_w_in"], np.float32),
        "l1_w_out": np.ascontiguousarray(inp["l1_w_out"], np.float32),
    }
    lamv = np.concatenate([np.asarray(inp[k], np.float32).reshape(-1) for k in
                           ("l1_lambda_q1", "l1_lambda_k1", "l1_lambda_q2", "l1_lambda_k2")])
    maps = []
    for core in range(2 * B):
        b, rank = core // 2, core % 2
        s0 = rank * T
        xe = np.zeros((HALO + T, D), np.float32)
        if s0 > 0:
            xe[:] = x[b, s0 - HALO:s0 + T]
        else:
            xe[HALO:] = x[b, 0:T]
        cA, bcol, icnt, tfull = host_tables(cfg, core)
        cvec = np.zeros((128, lay["_n"]), np.float32)

        def put(name, arr):
            o, n = lay[name]
            arr = np.asarray(arr, np.float32)
            assert arr.shape[-1] == n, (name, arr.shape, n)
            cvec[:, o:o + n] = arr
        put("g0", fm(inp["l0_norm_g"]))
        put("gm0", fm(inp["l0_mem_norm_g"]))
        put("g1", fm(inp["l1_norm_g"]))
        put("gm1", fm(inp["l1_mem_norm_g"]))
        put("gf", fm(inp["final_norm_g"]))
        put("pscale", fm(inp["l0_pool_scale"]))
        put("subg", fm(inp["l1_subln_g"]))
        put("cA", cA[None, :])
        put("bcol", bcol[None, :])
        put("icnt", icnt[None, :])
        put("tfull", tfull)
        put("lamv", lamv[None, :])
        m = {"xT": np.ascontiguousarray(xe.T), "memT": np.ascontiguousarray(mem[b].T), "cvec": cvec}
        DMB_, HK_ = cfg["DMB"], cfg["DMB"] // 2
        for nm in ("l0_w_mem_kv", "l1_w_mem_kv"):
            wkv = np.asarray(inp[nm], np.float32)
            m[nm] = np.ascontiguousarray(np.concatenate(
                [wkv[:, rank * HK_:(rank + 1) * HK_], wkv[:, DMB_ + rank * HK_: DMB_ + (rank + 1) * HK_]], axis=1))
        m.update(shared)
        maps.append(m)
    return maps


def run(cfg, inp, debug=False, trace=False):
    nc = Builder(cfg, debug=debug).build()
    maps = prep_inputs(cfg, inp)
    res = run_bass_kernel_spmd(nc, maps, core_ids=list(range(2 * cfg["B"])), **({"trace": True} if trace else {}))
    T, S, B, D = cfg["T"], cfg["S"], cfg["B"], cfg["D"]
    out = np.empty((B, S, D), np.float32)
    for core in range(2 * B):
        b, rank = core // 2, core % 2
        out[b, rank * T:(rank + 1) * T, :] = res.results[core]["outT"].T
    return out, res


def kernel(**inputs):
    out, _ = run(CFG_FULL, inputs)
    return out
```
